# Optimizing a Trainium2 kernel written in Bass

```python
import jax, jax.numpy as jnp
from jax import lax
import numpy as np

D_MODEL = 1024
BATCH = 8
SEQ = 2048
DEPTH = 4
DEC_BATCH = 128
DEC_SEQ = 8
PAST_LEN = 16384
PAGE_SIZE = 128

N_PAIRS = DEPTH // 2
A_WIDTH = D_MODEL // 2
CONV_WIDTH = 3
B_WIDTH = D_MODEL // 2
HEAD_SIZE = 64
B_HEADS = B_WIDTH // HEAD_SIZE
DECAY_LORA = 64
AAA_LORA = 64
C_WIDTH = D_MODEL // 2
POOL_WINDOWS = (2, 4, 8, 16)
POOL_GROUPS = len(POOL_WINDOWS)
POOL_GROUP_DIM = C_WIDTH // POOL_GROUPS
POOL_HIST = max(POOL_WINDOWS) - 1
D_WIDTH = D_MODEL // 2
CHUNK = 128
D_GROUPS = 4
D_GROUP_DIM = D_WIDTH // D_GROUPS
SHIFT_COLS = 3 * B_WIDTH + DECAY_LORA + AAA_LORA
EVEN_COLS = 4 * A_WIDTH + SHIFT_COLS + B_WIDTH
ODD_COLS = 2 * C_WIDTH + 3 * D_WIDTH
RMS_EPS = 1e-6
LN_EPS = 1e-5
GN_EPS = 64e-5

kernel_name = "hybrid_conv_rwkv7_pool_chunkmlp_step"


def rmsnorm(x, g):
    x32 = x.astype(jnp.float32)
    y = x32 * lax.rsqrt(jnp.mean(x32 * x32, axis=-1, keepdims=True) + RMS_EPS)
    return (y * g.astype(jnp.float32)).astype(x.dtype)


def short_conv(u, hist, w):
    T = u.shape[1]
    ext = jnp.concatenate([hist.astype(u.dtype), u], axis=1)
    y = w[0] * ext[:, 0:T] + w[1] * ext[:, 1:T + 1] + w[2] * ext[:, 2:T + 2]
    return y, ext[:, T:]


def token_shift(q, prev, mu):
    prev_seq = jnp.concatenate([prev[:, None].astype(q.dtype), q[:, :-1]], axis=1)
    return q + mu * (prev_seq - q), q[:, -1]


def wkv7_scan(r, decay, k, v, kk, a, S0):
    def step(S, inp):
        r_t, w_t, k_t, v_t, kk_t, a_t = inp
        sa = jnp.einsum('bhij,bhj->bhi', S, -kk_t)
        S = (S * w_t[:, :, None, :] + sa[..., None] * (kk_t * a_t)[:, :, None, :]
             + v_t[..., None] * k_t[:, :, None, :])
        return S, jnp.einsum('bhij,bhj->bhi', S, r_t)
    xs = tuple(jnp.moveaxis(t.astype(jnp.float32), 1, 0) for t in (r, decay, k, v, kk, a))
    S, ys = lax.scan(step, S0.astype(jnp.float32), xs)
    return jnp.moveaxis(ys, 0, 1), S


def pool_mix(p, hist, past_len):
    T = p.shape[1]
    ext = jnp.concatenate([hist.astype(p.dtype), p], axis=1).astype(jnp.float32)
    cs = jnp.cumsum(ext, axis=1)
    cs = jnp.concatenate([jnp.zeros_like(cs[:, :1]), cs], axis=1)
    end = POOL_HIST + 1
    pos = past_len + jnp.arange(T, dtype=jnp.float32)
    outs = []
    for g, w in enumerate(POOL_WINDOWS):
        sl = slice(g * POOL_GROUP_DIM, (g + 1) * POOL_GROUP_DIM)
        win = cs[:, end:end + T, sl] - cs[:, end - w:end - w + T, sl]
        cnt = jnp.minimum(jnp.float32(w), pos + 1.0)[None, :, None]
        outs.append(win / cnt - ext[:, POOL_HIST:, sl])
    return jnp.concatenate(outs, axis=-1).astype(p.dtype), ext[:, T:].astype(p.dtype)


def even_layer(h, conv_hist, shift_prev, wkv_state, w_in, conv_w, shift_mu, w0, w2, a0, a2,
               k_k, k_a, r_k, gn_g, gn_b, w_out):
    Bsz, T, _ = h.shape
    proj = h @ w_in
    a_h, a_b, a_c, a_z, shift_in, b_z = jnp.split(
        proj, [A_WIDTH, 2 * A_WIDTH, 3 * A_WIDTH, 4 * A_WIDTH, 4 * A_WIDTH + SHIFT_COLS], axis=-1)
    conv_out, new_conv = short_conv(a_c * a_h, conv_hist, conv_w)
    y_a = a_b * conv_out * jax.nn.silu(a_z)
    shifted, new_shift = token_shift(shift_in, shift_prev, shift_mu)
    r, k, v, wl, al = jnp.split(
        shifted, [B_WIDTH, 2 * B_WIDTH, 3 * B_WIDTH, 3 * B_WIDTH + DECAY_LORA], axis=-1)
    w_log = (-jax.nn.softplus(-(w0 + jnp.tanh(wl) @ w2)) - 0.5).astype(jnp.float32)
    decay = jnp.exp(-jnp.exp(w_log))
    a = jax.nn.sigmoid(a0 + al @ a2)
    heads = lambda t: t.reshape(Bsz, T, B_HEADS, HEAD_SIZE)
    kk = heads(k * k_k).astype(jnp.float32)
    kk = kk / jnp.maximum(jnp.sqrt(jnp.sum(kk * kk, axis=-1, keepdims=True)), 1e-12)
    k = k * (1 + (a - 1) * k_a)
    rh, kh, vh, ah = heads(r), heads(k), heads(v), heads(a)
    y, new_wkv = wkv7_scan(rh, heads(decay), kh, vh, kk, ah, wkv_state)
    mean = jnp.mean(y, axis=-1, keepdims=True)
    var = jnp.mean(jnp.square(y - mean), axis=-1, keepdims=True)
    y = (y - mean) * lax.rsqrt(var + GN_EPS)
    y = y.reshape(Bsz, T, B_WIDTH) * gn_g.astype(jnp.float32) + gn_b.astype(jnp.float32)
    bonus = jnp.sum((rh * kh * r_k).astype(jnp.float32), axis=-1, keepdims=True) * vh.astype(jnp.float32)
    y = (y + bonus.reshape(Bsz, T, B_WIDTH)).astype(h.dtype)
    y_b = y * jax.nn.silu(b_z)
    out = jnp.concatenate([y_a, y_b], axis=-1) @ w_out
    return out, new_conv, new_shift, new_wkv


def odd_layer(h, pool_hist, past_len, w_in, pool_w, pool_scale, v_ln_g, v_ln_b,
              spatial_w, spatial_b, w_out):
    Bsz, T, _ = h.shape
    proj = h @ w_in
    p, c_z, u, v, d_z = jnp.split(
        proj, [C_WIDTH, 2 * C_WIDTH, 2 * C_WIDTH + D_WIDTH, 2 * C_WIDTH + 2 * D_WIDTH], axis=-1)
    pooled, new_pool = pool_mix(p, pool_hist, past_len)
    pooled = pooled.reshape(Bsz, T, POOL_GROUPS, POOL_GROUP_DIM)
    y_c = jnp.einsum('btgc,gcd->btgd', pooled, pool_w).reshape(Bsz, T, C_WIDTH)
    y_c = y_c * pool_scale * jax.nn.silu(c_z)
    v32 = v.astype(jnp.float32)
    mu = jnp.mean(v32, axis=-1, keepdims=True)
    var = jnp.mean(jnp.square(v32 - mu), axis=-1, keepdims=True)
    vn = ((v32 - mu) * lax.rsqrt(var + LN_EPS) * v_ln_g.astype(jnp.float32)
          + v_ln_b.astype(jnp.float32)).astype(h.dtype)
    L = min(T, CHUNK)
    n_chunks = T // L
    mask = jnp.tril(jnp.ones((L, L), dtype=bool))
    w_s = jnp.where(mask[None], spatial_w[:, :L, :L], 0)
    vc = vn.reshape(Bsz, n_chunks, L, D_GROUPS, D_GROUP_DIM)
    mixed = jnp.einsum('gts,bnsgc->bntgc', w_s, vc) + spatial_b[:, :L].T[:, :, None]
    y_d = u * mixed.reshape(Bsz, T, D_WIDTH) * jax.nn.silu(d_z)
    out = jnp.concatenate([y_c, y_d], axis=-1) @ w_out
    return out, new_pool, vn


def setup_inputs(seed: int = 0) -> dict:
    key = jax.random.key(seed)
    ks = jax.random.split(key, 32)
    nrm = lambda i, shape, s=1.0: s * jax.random.normal(ks[i], shape, jnp.float32)
    return {
        "x_prompt": nrm(0, (BATCH, SEQ, D_MODEL)),
        "x_sample": nrm(1, (DEC_BATCH, DEC_SEQ, D_MODEL)),
        "state_conv": nrm(2, (N_PAIRS, DEC_BATCH, CONV_WIDTH - 1, A_WIDTH)),
        "state_shift": nrm(3, (N_PAIRS, DEC_BATCH, SHIFT_COLS)),
        "state_wkv": nrm(4, (N_PAIRS, DEC_BATCH, B_HEADS, HEAD_SIZE, HEAD_SIZE), 0.3),
        "state_pool": nrm(5, (N_PAIRS, DEC_BATCH, POOL_HIST, C_WIDTH)),
        "norm_w": 1.0 + nrm(6, (DEPTH, D_MODEL), 0.1),
        "final_norm_w": 1.0 + nrm(7, (D_MODEL,), 0.1),
        "w_in_even": nrm(8, (N_PAIRS, D_MODEL, EVEN_COLS), D_MODEL ** -0.5),
        "conv_w": nrm(9, (N_PAIRS, CONV_WIDTH, A_WIDTH), CONV_WIDTH ** -0.5),
        "shift_mu": jax.random.uniform(ks[10], (N_PAIRS, SHIFT_COLS), jnp.float32),
        "w0": 0.5 + nrm(11, (N_PAIRS, B_WIDTH), 0.5),
        "w2": nrm(12, (N_PAIRS, DECAY_LORA, B_WIDTH), 0.5 * DECAY_LORA ** -0.5),
        "a0": nrm(13, (N_PAIRS, B_WIDTH), 0.1),
        "a2": nrm(14, (N_PAIRS, AAA_LORA, B_WIDTH), 0.5 * AAA_LORA ** -0.5),
        "k_k": 0.85 + nrm(15, (N_PAIRS, B_WIDTH), 0.1),
        "k_a": 1.0 + nrm(16, (N_PAIRS, B_WIDTH), 0.1),
        "r_k": nrm(17, (N_PAIRS, B_HEADS, HEAD_SIZE), 0.1),
        "gn_g": 1.0 + nrm(18, (N_PAIRS, B_WIDTH), 0.1),
        "gn_b": nrm(19, (N_PAIRS, B_WIDTH), 0.01),
        "w_out_even": nrm(20, (N_PAIRS, A_WIDTH + B_WIDTH, D_MODEL), (A_WIDTH + B_WIDTH) ** -0.5),
        "w_in_odd": nrm(21, (N_PAIRS, D_MODEL, ODD_COLS), D_MODEL ** -0.5),
        "pool_w": nrm(22, (N_PAIRS, POOL_GROUPS, POOL_GROUP_DIM, POOL_GROUP_DIM), POOL_GROUP_DIM ** -0.5),
        "pool_scale": 1.0 + nrm(23, (N_PAIRS, C_WIDTH), 0.1),
        "v_ln_g": 1.0 + nrm(24, (N_PAIRS, D_WIDTH), 0.1),
        "v_ln_b": nrm(25, (N_PAIRS, D_WIDTH), 0.01),
        "spatial_w": nrm(26, (N_PAIRS, D_GROUPS, CHUNK, CHUNK), CHUNK ** -0.5),
        "spatial_b": 1.0 + nrm(27, (N_PAIRS, D_GROUPS, CHUNK), 0.1),
        "w_out_odd": nrm(28, (N_PAIRS, C_WIDTH + D_WIDTH, D_MODEL), (C_WIDTH + D_WIDTH) ** -0.5),
    }


def reference(x_prompt, x_sample, state_conv, state_shift, state_wkv, state_pool,
              norm_w, final_norm_w, w_in_even, conv_w, shift_mu, w0, w2, a0, a2, k_k, k_a, r_k,
              gn_g, gn_b, w_out_even, w_in_odd, pool_w, pool_scale, v_ln_g, v_ln_b,
              spatial_w, spatial_b, w_out_odd):
    xp, xs = x_prompt, x_sample
    conv_p, conv_s, shift_p, shift_s, wkv_p, wkv_s = [], [], [], [], [], []
    pool_p, pool_s, chunkv_s = [], [], []
    for layer in range(DEPTH):
        i = layer // 2
        hp = rmsnorm(xp, norm_w[layer])
        hs = rmsnorm(xs, norm_w[layer])
        if layer % 2 == 0:
            ew = (w_in_even[i], conv_w[i], shift_mu[i], w0[i], w2[i], a0[i], a2[i],
                  k_k[i], k_a[i], r_k[i], gn_g[i], gn_b[i], w_out_even[i])
            z_conv = jnp.zeros((BATCH, CONV_WIDTH - 1, A_WIDTH), xp.dtype)
            z_shift = jnp.zeros((BATCH, SHIFT_COLS), xp.dtype)
            z_wkv = jnp.zeros((BATCH, B_HEADS, HEAD_SIZE, HEAD_SIZE), jnp.float32)
            op, c1, s1, w1 = even_layer(hp, z_conv, z_shift, z_wkv, *ew)
            osm, c2, s2, w2_ = even_layer(hs, state_conv[i], state_shift[i], state_wkv[i], *ew)
            conv_p.append(c1); conv_s.append(c2)
            shift_p.append(s1); shift_s.append(s2)
            wkv_p.append(w1); wkv_s.append(w2_)
        else:
            ow = (w_in_odd[i], pool_w[i], pool_scale[i], v_ln_g[i], v_ln_b[i],
                  spatial_w[i], spatial_b[i], w_out_odd[i])
            z_pool = jnp.zeros((BATCH, POOL_HIST, C_WIDTH), xp.dtype)
            op, p1, _ = odd_layer(hp, z_pool, 0, *ow)
            osm, p2, v2 = odd_layer(hs, state_pool[i], PAST_LEN, *ow)
            pool_p.append(p1); pool_s.append(p2)
            chunkv_s.append(v2)
        xp = xp + op
        xs = xs + osm
    y_prompt = rmsnorm(xp, final_norm_w)
    y_sample = rmsnorm(xs, final_norm_w)
    return (y_prompt, y_sample,
            jnp.stack(conv_p), jnp.stack(conv_s),
            jnp.stack(shift_p), jnp.stack(shift_s),
            jnp.stack(wkv_p), jnp.stack(wkv_s),
            jnp.stack(pool_p), jnp.stack(pool_s),
            jnp.stack(chunkv_s))
```

```python
import numpy as np
import concourse.bass as bass
import concourse.mybir as mybir
from concourse.bass_utils import run_bass_kernel_spmd

F32 = mybir.dt.float32
BF16 = mybir.dt.bfloat16
AF = mybir.ActivationFunctionType
ALU = mybir.AluOpType
AX = mybir.AxisListType

NCORE = 8
D = 1024
SEQ = 2048
NS = 16
ST = 8
NTOK = SEQ + NS * ST
ECOLS = 4224
OCOLS = 2560
TBS = [(0, 512, 1, 512), (512, 512, 1, 512), (1024, 512, 1, 512), (1536, 512, 1, 512),
       (2048, 128, 16, 8)]
TGS = [[0, 1], [2, 3, 4]]
TGW = 1152
FW = 528


class _Op:
    __slots__ = ("eng", "fn", "reads", "writes", "is_dma", "deps", "signaled", "sem", "semval")

    def __init__(self, eng, fn, reads, writes, is_dma):
        self.eng = eng
        self.fn = fn
        self.reads = reads
        self.writes = writes
        self.is_dma = is_dma
        self.deps = []
        self.signaled = False
        self.sem = None
        self.semval = 0


class Prog:
    def __init__(self, nc, dma_pool=8):
        self.nc = nc
        self.ops = []
        self.eng_obj = {"pe": nc.tensor, "act": nc.scalar, "dve": nc.vector,
                        "pool": nc.gpsimd, "sp": nc.sync}
        self.dma_pool = dma_pool

    def op(self, eng, fn, reads=(), writes=()):
        reads = tuple(reads)
        writes = tuple(writes) + tuple(r for r in reads if r in PSUM_NAMES)
        self.ops.append(_Op(eng, fn, reads, writes, False))

    def dma(self, queue, fn, reads=(), writes=()):
        self.ops.append(_Op(queue, fn, tuple(reads), tuple(writes), True))

    def finalize(self):
        nc = self.nc
        ops = self.ops
        last_writer = {}
        readers = {}

        def need(prod, cons, kind):
            if prod.is_dma or cons.is_dma:
                return True
            if prod.eng != cons.eng:
                return True
            return prod.eng != "pe"

        dma_hist = {}
        for o in ops:
            deps = {}
            for b in o.reads:
                w = last_writer.get(b)
                if w is not None and need(w, o, "RAW"):
                    deps[id(w)] = w
            for b in o.writes:
                w = last_writer.get(b)
                if w is not None and need(w, o, "WAW"):
                    deps[id(w)] = w
                for r in readers.get(b, ()):
                    if r is not o and need(r, o, "WAR"):
                        deps[id(r)] = r
            if o.is_dma:
                h = dma_hist.setdefault(o.eng, [])
                if len(h) >= self.dma_pool:
                    p = h[len(h) - self.dma_pool]
                    deps[id(p)] = p
                h.append(o)
            for b in o.reads:
                readers.setdefault(b, []).append(o)
            for b in o.writes:
                last_writer[b] = o
                readers[b] = []
            o.deps = list(deps.values())
            for d in o.deps:
                d.signaled = True
        self.sems = {e: nc.alloc_semaphore("s_" + e) for e in self.eng_obj}
        dma_sems = {}
        cnt = {e: 0 for e in self.eng_obj}
        dcnt = {}
        dma_i = {}
        for o in ops:
            if o.is_dma:
                k = dma_i.get(o.eng, 0)
                dma_i[o.eng] = k + 1
                slot = (o.eng, k % self.dma_pool)
                if slot not in dma_sems:
                    dma_sems[slot] = nc.alloc_semaphore("d_%s_%d" % slot)
                dcnt[slot] = dcnt.get(slot, 0) + 16
                o.sem = dma_sems[slot]
                o.semval = dcnt[slot]
                o.signaled = True
            elif o.signaled:
                cnt[o.eng] += 1
                o.sem = self.sems[o.eng]
                o.semval = cnt[o.eng]
        waited = {e: {} for e in self.eng_obj}
        n_wait = 0
        for o in ops:
            eng = self.eng_obj[o.eng]
            wd = waited[o.eng]
            req = {}
            for d in o.deps:
                key = id(d.sem)
                if key not in req or req[key][1] < d.semval:
                    req[key] = (d.sem, d.semval)
            for key, (sem, val) in req.items():
                if wd.get(key, 0) >= val:
                    continue
                eng.wait_ge(sem, val)
                wd[key] = val
                n_wait += 1
            ins = o.fn(eng)
            if o.signaled:
                ins.then_inc(o.sem, 16 if o.is_dma else 1)
        for q, h in dma_hist.items():
            eng = self.eng_obj[q]
            last = {}
            for o in h:
                last[id(o.sem)] = o
            for o in last.values():
                eng.wait_ge(o.sem, o.semval)
        return len(ops), n_wait, max(cnt.values())


class B:
    __slots__ = ("ap", "names")

    def __init__(self, ap, names):
        self.ap = ap
        self.names = names if isinstance(names, tuple) else (names,)

    def __getitem__(self, k):
        return B(self.ap[k], self.names)

    def v(self, ap):
        return B(ap, self.names)

    def re(self, pat, **kw):
        return B(self.ap.rearrange(pat, **kw), self.names)

    def bc(self, shape):
        return B(self.ap.to_broadcast(list(shape)), self.names)

    def un(self, axis):
        return B(self.ap.unsqueeze(axis), self.names)

    def bitcast(self, dt):
        return B(self.ap.bitcast(dt), self.names)


def _nm(*xs):
    out = []
    for x in xs:
        if isinstance(x, B):
            out.extend(x.names)
    return out


def _ap(x):
    return x.ap if isinstance(x, B) else x


PSUM_NAMES = {"pj0", "px0", "px1", "pw0", "pw1", "pw2", "pw3", "py"}


class _Stop(Exception):
    pass


_END = object()


LIMIT = None


class K:
    def cp(self, name):
        if LIMIT is not None and name == LIMIT:
            raise _Stop()

    def __init__(self):
        self.nc = bass.Bass("TRN2", target_bir_lowering=False)
        self.P = Prog(self.nc)
        self.n_alloc = 0
        self.rr = 0
        self.pj_i = 0
        self.pw_i = 0
        self.wslot_i = 0
        self.chunk_ctr = 0

    def sb(self, name, shape, dt=F32):
        return B(self.nc.alloc_sbuf_tensor(name, list(shape), dt).ap(), name)

    def dram_in(self, name, shape, dt=F32):
        return self.nc.dram_tensor(name, list(shape), dt, kind="ExternalInput").ap()

    def dram_out(self, name, shape, dt=F32):
        return self.nc.dram_tensor(name, list(shape), dt, kind="ExternalOutput").ap()

    def tt(self, eng, out, a, b, op):
        self.P.op(eng, lambda e: e.tensor_tensor(out=out.ap, in0=a.ap, in1=b.ap, op=op),
                  reads=_nm(a, b), writes=_nm(out))

    def ts(self, eng, out, a, s1, op0, s2=None, op1=None):
        if op1 is None:
            self.P.op(eng, lambda e: e.tensor_scalar(out=out.ap, in0=a.ap, scalar1=_ap(s1), scalar2=None, op0=op0),
                      reads=_nm(a, s1), writes=_nm(out))
        else:
            self.P.op(eng, lambda e: e.tensor_scalar(out=out.ap, in0=a.ap, scalar1=_ap(s1), scalar2=_ap(s2),
                                                     op0=op0, op1=op1),
                      reads=_nm(a, s1, s2), writes=_nm(out))

    def stt(self, eng, out, a, s, b, op0, op1):
        self.P.op(eng, lambda e: e.scalar_tensor_tensor(out=out.ap, in0=a.ap, scalar=_ap(s), in1=b.ap,
                                                        op0=op0, op1=op1),
                  reads=_nm(a, s, b), writes=_nm(out))

    def act(self, out, a, func, scale=1.0, bias=0.0, accum=None):
        if accum is None:
            self.P.op("act", lambda e: e.activation(out=out.ap, in_=a.ap, func=func, bias=_ap(bias), scale=_ap(scale)),
                      reads=_nm(a, scale, bias), writes=_nm(out))
        else:
            self.P.op("act", lambda e: e.activation(out=out.ap, in_=a.ap, func=func, bias=_ap(bias), scale=_ap(scale),
                                                    accum_out=accum.ap),
                      reads=_nm(a, scale, bias), writes=_nm(out, accum))

    def copy(self, eng, out, a):
        if eng == "act":
            self.P.op("act", lambda e: e.copy(out=out.ap, in_=a.ap), reads=_nm(a), writes=_nm(out))
        else:
            self.P.op(eng, lambda e: e.tensor_copy(out=out.ap, in_=a.ap), reads=_nm(a), writes=_nm(out))

    def anycopy(self, out, a):
        self.rr += 1
        self.copy("act" if self.rr % 2 else "dve", out, a)

    def memset(self, eng, out, val):
        self.P.op(eng, lambda e: e.memset(out.ap, val), writes=_nm(out))

    def recip(self, out, a):
        self.P.op("dve", lambda e: e.reciprocal(out=out.ap, in_=a.ap), reads=_nm(a), writes=_nm(out))

    def rsum(self, out, a):
        self.P.op("dve", lambda e: e.reduce_sum(out=out.ap, in_=a.ap, axis=AX.X), reads=_nm(a), writes=_nm(out))

    def scan(self, out, d0, d1):
        self.P.op("dve", lambda e: e.tensor_tensor_scan(out=out.ap, data0=d0.ap, data1=d1.ap, initial=0.0,
                                                        op0=ALU.mult, op1=ALU.add),
                  reads=_nm(d0, d1), writes=_nm(out))

    def mm(self, out, lhsT, rhs, start=True, stop=True):
        self.P.op("pe", lambda e: e.matmul(out.ap, lhsT=lhsT.ap, rhs=rhs.ap, start=start, stop=stop),
                  reads=_nm(lhsT, rhs), writes=_nm(out))

    def tr(self, out, a, ident):
        self.P.op("pe", lambda e: e.transpose(out=out.ap, in_=a.ap, identity=ident.ap),
                  reads=_nm(a, ident), writes=_nm(out))

    def dma(self, q, out, a, slow=False):
        if slow:
            self.P.dma(q, lambda e: e.dma_start(out=_ap(out), in_=_ap(a), allow_slow_non_contiguous=True),
                       reads=_nm(a), writes=_nm(out))
        else:
            self.P.dma(q, lambda e: e.dma_start(out=_ap(out), in_=_ap(a)), reads=_nm(a), writes=_nm(out))

    def pj(self):
        self.pj_i += 1
        return self.PJ[self.pj_i % len(self.PJ)]

    def px(self):
        self.px_i += 1
        return self.PX[self.px_i % len(self.PX)]

    def pw(self):
        self.pw_i += 1
        return self.PW[self.pw_i % len(self.PW)]

    def wload(self, wdram, col_blocks, width):
        self.wslot_i += 1
        slot = self.WS[self.wslot_i % len(self.WS)]
        wv = wdram.rearrange("(k p) n -> p k n", p=128)
        for j, c0 in enumerate(col_blocks):
            self.dma("pool", slot[:, :, j * width:(j + 1) * width], wv[:, :, c0:c0 + width])
        return slot

    def build(self):
        nc = self.nc
        d = {}
        d["xtok"] = self.dram_in("xtok", [17, 128, D])
        d["w_in_even"] = self.dram_in("w_in_even", [2, D, ECOLS])
        d["w_out_even"] = self.dram_in("w_out_even", [2, D, D])
        d["w_in_odd"] = self.dram_in("w_in_odd", [2, D, OCOLS])
        d["w_out_odd"] = self.dram_in("w_out_odd", [2, D, D])
        d["consts"] = self.dram_in("consts", [128, CONST_COLS])
        d["pe"] = self.dram_in("pe", [2, 128, NPE])
        d["po"] = self.dram_in("po", [2, 128, NPO])
        d["lora2"] = self.dram_in("lora2", [2, 128, 512])
        d["fnw"] = self.dram_in("fnw", [128, 8])
        d["sconv"] = self.dram_in("sconv", [2, 128, 4 * NS * 2])
        d["sshift"] = self.dram_in("sshift", [2, 128, 13 * NS])
        d["spool"] = self.dram_in("spool", [2, 128, 4, NS * 15])
        d["swkv"] = self.dram_in("swkv", [2, 4, 128, NS * 64])
        d["lnrow"] = self.dram_in("lnrow", [2, 2 * 512 + 512])
        d["poolw"] = self.dram_in("poolw", [2, 128, 4 * 128])
        d["spT"] = self.dram_in("spT", [2, 128, 4 * 128])
        o = {}
        o["ytok"] = self.dram_out("ytok", [17, 128, D])
        o["oconv"] = self.dram_out("oconv", [2, 128, 4 * 17 * 2])
        o["oshift"] = self.dram_out("oshift", [2, 128, 13 * 17])
        o["owkv"] = self.dram_out("owkv", [2, 4, 128, 17 * 64])
        o["opool"] = self.dram_out("opool", [2, 4, 128, 17 * 15])
        o["ochunkv"] = self.dram_out("ochunkv", [2, 128, 512])
        self.d, self.o = d, o

        self.xF = self.sb("xF", [128, 8, NTOK])
        self.hT = self.sb("hT", [128, 8, TGW], BF16)
        self.yTa = self.sb("yTa", [128, 4, TGW], BF16)
        self.yTb = self.sb("yTb", [128, 4, TGW], BF16)
        self.ysrc = [self.yTa[:, k, :] for k in range(4)] + [self.yTb[:, k, :] for k in range(4)]
        self.WS = [self.sb("ws%d" % i, [128, 8, 512], BF16) for i in range(2)]
        self.cst = self.sb("cst", [128, 128 + CONST_COLS - CB_COLS])
        fbig = self.nc.alloc_sbuf_tensor("fbig", [128, 16 * FW], F32).ap()
        self.f = [B(fbig[:, i * FW:(i + 1) * FW], "f%d" % i) for i in range(16)]
        self.mixT = B(fbig[:, 8 * FW:8 * FW + 4 * TGW // 2].bitcast(BF16).rearrange("p (g t) -> p g t", g=4),
                      ("f8", "f9", "f10", "f11", "f12"))
        self.lnrow = B(fbig[:, 13 * FW:13 * FW + 1536], ("f13", "f14", "f15"))
        self.b = [self.sb("b%d" % i, [128, FW], BF16) if i != 1 else None for i in range(9)]
        self.QRF = self.sb("QRF", [128, 2, 512], BF16)
        ya = self.yTa.ap.rearrange("p g t -> p (g t)")

        def yab(o0, w):
            return B(ya[:, o0:o0 + w], "yTa")
        self.psets = [
            dict(QRF=self.QRF, PF=self.b[4], KF=self.b[5], PhF=self.b[6], KhF=self.b[7], VF=self.b[8],
                 gz=self.b[2], eL=self.f[13], bonus=self.f[11],
                 sqf=B(self.QRF.ap.rearrange("p a t -> p (a t)").bitcast(F32), "QRF")),
            dict(QRF=B(ya[:, 0:1024].rearrange("p (a t) -> p a t", a=2), "yTa"),
                 PF=yab(1024, FW), KF=yab(1024 + FW, FW), PhF=yab(1024 + 2 * FW, FW), KhF=yab(1024 + 3 * FW, FW),
                 VF=yab(1024 + 4 * FW, FW), gz=yab(1024 + 5 * FW, FW), eL=self.f[8], bonus=self.f[14],
                 sqf=B(ya[:, 0:1024].bitcast(F32), "yTa")),
        ]
        self.lora = self.sb("lora", [128, TGW], BF16)
        self.lora2 = self.sb("lora2_s", [128, 512], BF16)
        self.prm = self.sb("prm", [128, max(NPE, NPO) + 16])
        self.fnw = self.sb("fnw_s", [128, 8])
        self.poolw = self.sb("poolw_s", [128, 4, 128], BF16)
        self.WsT = self.sb("WsT", [128, 4, 128], BF16)
        self.WsTs = self.sb("WsTs", [128, 4, 128], BF16)
        self.cc = self.sb("cc", [128, 4, 2])
        self.sc = self.sb("sc", [128, 13])
        self.pc = self.sb("pc", [128, 4, 15])
        self.Hp = self.sb("Hp", [128, 4, 64])
        self.Hs = self.sb("Hs", [128, NS, 64])
        self.Hb = self.sb("Hb", [128, NS, 64], BF16)
        self.Hbd = self.sb("Hbd", [128, 128], BF16)
        self.sconv = self.sb("sconv_s", [128, 4, NS, 2])
        self.sshift = self.sb("sshift_s", [128, 13, NS])
        self.spool = self.sb("spool_s", [128, NS, 15])
        self.oconv = self.sb("oconv_s", [128, 4, 17, 2])
        self.oshift = self.sb("oshift_s", [128, 13, 17])
        self.small = self.sb("small", [128, 64])
        self.tok3s = [self.sb("tok3_%d" % i, [128, 3, 128], BF16) for i in range(3)]
        self.AMs = [self.sb("AM_%d" % i, [128, 2, 4, 128], BF16) for i in range(4)]
        self.tok3s.append(B(self.spool.ap.rearrange("p b r -> p (b r)").bitcast(BF16)[:, 0:384]
                            .rearrange("p (a t) -> p a t", a=3), "spool_s"))
        self.BIGs = [self.sb("NS%d" % st, [128, 2, 2, 128], BF16) for st in range(4)]
        self.Npings = [[self.sb("Nping%d_%d" % (st, i), [128, 2, 128], BF16) for i in range(2)] for st in range(3)]
        self.Npings.append([self.WsT[:, 0:2, :], self.WsT[:, 2:4, :]])
        self.ZT = self.sb("ZT", [64, 2, 2, 128], BF16)
        self.X = [self.sb("X%d" % i, [128, 128], BF16) for i in range(2)]
        self.UmVm = []
        for si in range(4):
            am = self.AMs[si].ap.rearrange("p h a t -> p (h a t)")
            self.UmVm.append((B(am[:, 0:512].rearrange("p (s t) -> p s t", s=4), "AM_%d" % si),
                              B(am[:, 512:1024].rearrange("p (s t) -> p s t", s=4), "AM_%d" % si)))
        self.cb = self.sb("cb", [128, CB_COLS], BF16)
        self.PJ2 = [B(nc.alloc_psum_tensor("pj%d" % i, [128, 512], F32).ap(), "pj%d" % i) for i in range(1)]
        self.PX2 = [B(nc.alloc_psum_tensor("px%d" % i, [128, 512], F32).ap(), "px%d" % i) for i in range(2)]
        self.PW = [B(nc.alloc_psum_tensor("pw%d" % i, [128, 512], F32).ap(), "pw%d" % i) for i in range(4)]
        self.PJ = self.PJ2 + self.PX2
        self.PX = self.PX2
        self.px_i = 0
        self.PY = B(nc.alloc_psum_tensor("py", [128, 512], F32).ap(), "py")

        self.dma("sp", self.cst[:, 0:128], d["consts"][:, 0:128])
        self.dma("sp", self.cst[:, 128:], d["consts"][:, CB_COLS:CONST_COLS])
        self.dma("pool", self.cb, d["consts"][:, 0:CB_COLS])
        self.dma("sp", self.fnw, d["fnw"])
        c = self.cst
        FO = 128 - CB_COLS
        self.identf = c[:, 0:128]
        self.rstp = c[:, FO + C_RSTP:FO + C_RSTP + 512]
        self.rsts = c[:, FO + C_RSTS:FO + C_RSTS + 128]
        self.invcnt = c[:, FO + C_INVC:FO + C_INVC + 15]
        cb = self.cb
        self.identb = cb[:, C_IDENT:C_IDENT + 128]
        self.onesb = cb[:, C_ONES:C_ONES + 128]
        self.blk64 = cb[:, C_BLK64:C_BLK64 + 128]
        self.maskA = [cb[:, C_MASKA_P:C_MASKA_P + 512], cb[:, C_MASKA_S:C_MASKA_S + 512]]
        self.maskN = [cb[:, C_MASKN_P:C_MASKN_P + 256], cb[:, C_MASKN_S:C_MASKN_S + 256]]
        self.segm = cb[:, C_SEGM:C_SEGM + NS]
        self.rep8 = c[0:8, FO + C_REP8:FO + C_REP8 + 128]

        self.memset("dve", self.Hbd, 0.0)
        st0 = B(fbig[:, 0:1024], ("f0", "f1"))
        st1 = B(fbig[:, 2 * FW:2 * FW + 1024], ("f2", "f3"))
        stages = [st0, st1]
        for j in range(17):
            st = stages[j % 2]
            self.dma("sp", st, d["xtok"][j])
            for half in range(2):
                ps = self.pj()
                for q in range(4):
                    k = half * 4 + q
                    self.mm(ps[:, q * 128:(q + 1) * 128], st[:, k * 128:(k + 1) * 128], self.identf)
                self.anycopy(self.xF[:, half * 4:half * 4 + 4, j * 128:(j + 1) * 128],
                             ps.re("p (q t) -> p q t", q=4))

        try:
            self.cp("xload")
            for layer in range(4):
                if layer % 2 == 0:
                    self.even_layer(layer // 2)
                else:
                    self.odd_layer(layer // 2)
                self.cp("layer%d" % layer)
        except _Stop:
            pass

        for tg in TGS:
            for tbi in tg:
                g0, n, nseg, sl = TBS[tbi]
                rstd = self.norm_rstd(g0, n)
                for c0 in range(0, n, 128):
                    j = (g0 + c0) // 128
                    yfv = (B(fbig[:, 4 * FW:4 * FW + 1024].rearrange("p (k t) -> p k t", k=8), ("f4", "f5")),
                           B(fbig[:, 8 * FW:8 * FW + 1024].rearrange("p (k t) -> p k t", k=8), ("f8", "f9")))[j % 2]
                    for k in range(8):
                        self.stt("dve", yfv[:, k, :], self.xF[:, k, g0 + c0:g0 + c0 + 128], self.fnw[:, k:k + 1],
                                 rstd[:, c0:c0 + 128], ALU.mult, ALU.mult)
                    st = stages[j % 2]
                    for half in range(2):
                        ps = self.pj()
                        for q in range(4):
                            k = half * 4 + q
                            self.mm(ps[:, q * 128:(q + 1) * 128], yfv[:, k, :], self.identf)
                        self.copy("act", st[:, half * 512:(half + 1) * 512], ps)
                    self.dma("sp", o["ytok"][j], st)
        return self.P.finalize()

    def norm_rstd(self, g0, n):
        ps = self.pj()
        for k in range(8):
            sq = self.b[(0, 3)[k % 2]][:, 0:n]
            self.act(sq, self.xF[:, k, g0:g0 + n], AF.Square)
            self.mm(ps[:, 0:n], self.onesb, sq, start=(k == 0), stop=(k == 7))
        rstd = self.f[6][:, 0:n]
        self.act(rstd, ps[:, 0:n], AF.Ln, scale=1.0 / D, bias=1e-6)
        self.act(rstd, rstd, AF.Exp, scale=-0.5)
        return rstd

    def norm(self, tg, gcol):
        lc = 0
        for tbi in tg:
            g0, n, nseg, sl = TBS[tbi]
            rstd = self.norm_rstd(g0, n)
            for k in range(8):
                self.stt("dve", self.hT[:, k, lc:lc + n], self.xF[:, k, g0:g0 + n], gcol[:, k:k + 1], rstd,
                         ALU.mult, ALU.mult)
            lc += n

    def proj(self, W, j, width, lc, n, src=None):
        ps = self.pj()
        for k in range(8):
            sk = self.hT[:, k, lc:lc + n] if src is None else src[k][:, lc:lc + n]
            self.mm(ps[:, 0:n], W[:, k, j * width:(j + 1) * width], sk,
                    start=(k == 0), stop=(k == 7))
        return ps

    def outproj(self, tg, wdram):
        for dblk in range(8):
            W = self.wload(wdram, [dblk * 128], 128)
            lc = 0
            for tbi in tg:
                g0, n, nseg, sl = TBS[tbi]
                ps = self.proj(W, 0, 128, lc, n, src=self.ysrc)
                xs = self.xF[:, dblk, g0:g0 + n]
                self.tt("dve", xs, xs, ps[:, 0:n], ALU.add)
                lc += n

    def shift_evac(self, ps, tbi, chunk, dst, mu):
        g0, n, nseg, sl = TBS[tbi]
        qraw = self.f[0]
        dtmp = self.f[1]
        q3 = qraw[:, 0:nseg * (sl + 1)].re("p (s l) -> p s l", l=sl + 1)
        self.copy("act", q3[:, :, 1:sl + 1], ps[:, 0:n].re("p (s l) -> p s l", l=sl))
        if nseg == 1:
            self.copy("dve", q3[:, 0, 0:1], self.sc[:, chunk:chunk + 1])
        else:
            self.copy("dve", q3[:, :, 0], self.sshift[:, chunk, :])
        d3 = dtmp[:, 0:n].re("p (s l) -> p s l", l=sl)
        self.tt("dve", d3, q3[:, :, 0:sl], q3[:, :, 1:sl + 1], ALU.subtract)
        self.stt("dve", dst[:, 0:n].re("p (s l) -> p s l", l=sl), d3, mu[:, chunk:chunk + 1], q3[:, :, 1:sl + 1],
                 ALU.mult, ALU.add)
        if nseg == 1:
            self.copy("act", self.sc[:, chunk:chunk + 1], q3[:, 0, sl:sl + 1])
            if g0 + n == SEQ:
                self.copy("act", self.oshift[:, chunk, 0:1], q3[:, 0, sl:sl + 1])
        else:
            self.copy("act", self.oshift[:, chunk, 1:17], q3[:, :, sl])

    def even_layer(self, i):
        d, o = self.d, self.o
        prm = self.prm
        self.dma("sp", prm[:, 0:NPE], d["pe"][i])
        self.dma("pool", self.lora2, d["lora2"][i])
        self.dma("sp", self.sconv, d["sconv"][i].rearrange("p (g b r) -> p g b r", g=4, b=NS))
        self.dma("sp", self.sshift, d["sshift"][i].rearrange("p (c b) -> p c b", c=13))
        normw = prm[:, PE_NORM:PE_NORM + 8]
        cw = prm[:, PE_CW:PE_CW + 12].re("p (g j) -> p g j", g=4)
        mu = prm[:, PE_MU:PE_MU + 13]
        w0 = prm[:, PE_W0:PE_W0 + 4]
        a0 = prm[:, PE_A0:PE_A0 + 4]
        kkw = prm[:, PE_KK:PE_KK + 4]
        ka = prm[:, PE_KA:PE_KA + 4]
        rk = prm[:, PE_RK:PE_RK + 4]
        gng = prm[:, PE_GNG:PE_GNG + 4]
        gnb = prm[:, PE_GNB:PE_GNB + 4]
        base = max(NPE, NPO)
        negw0 = prm[:, base:base + 4]
        omka = prm[:, base + 4:base + 8]
        self.ts("dve", negw0, w0, -1.0, ALU.mult)
        nega0 = prm[:, base + 8:base + 12]
        self.ts("dve", nega0, a0, -1.0, ALU.mult)
        a0 = nega0
        self.ts("dve", omka, ka, -1.0, ALU.mult, 1.0, ALU.add)
        self.memset("dve", self.cc, 0.0)
        self.memset("dve", self.sc, 0.0)
        self.memset("dve", self.Hp, 0.0)
        wdr = d["w_in_even"][i]
        for tg in TGS:
            self.norm(tg, normw)
            self.cp("norm")
            W = self.wload(wdr, [3584], 128)
            lc = 0
            for tbi in tg:
                g0, n, nseg, sl = TBS[tbi]
                ps = self.proj(W, 0, 128, lc, n)
                sh = self.f[2]
                self.shift_evac(ps, tbi, 12, sh, mu)
                self.act(self.lora[0:64, lc:lc + n], sh[0:64, 0:n], AF.Tanh)
                self.copy("dve", self.lora[64:128, lc:lc + n], sh[64:128, 0:n])
                lc += n
            self.cp("lora")
            prm_ = (mu, negw0, a0, kkw, ka, omka, rk, gng, gnb)
            self.PJ, self.PX = self.PJ2, self.PX2
            items = []
            for pb in range(4):
                lc = 0
                for tbi in tg:
                    items.append([pb, tbi, None, lc])
                    lc += TBS[tbi][1]

            def wl(pb):
                return self.wload(wdr, [2048 + 128 * pb, 2560 + 128 * pb, 3072 + 128 * pb, 3712 + 128 * pb], 128)

            Wcur = {}

            def getW(k):
                if k < len(items):
                    pbk = items[k][0]
                    if pbk not in Wcur:
                        Wcur[pbk] = wl(pbk)
                    items[k][2] = Wcur[pbk]

            def prep(k, part):
                if k >= len(items):
                    return None
                return self.mixB_prep(items[k], self.psets[k % 2], prm_, part)

            getW(0)
            getW(1)
            self.run_streams(self.chain(prep(0, "E"), prep(0, "L")))
            flags = [[False] for _ in items]
            flags[0][0] = True
            states = {0: self.mixB_state(items[0], self.psets[0])}
            pending_epi = None
            for n_ in range(len(items)):
                it = items[n_]
                nxt = items[n_ + 1] if n_ + 1 < len(items) else None
                getW(n_ + 2)
                if nxt is not None:
                    states[n_ + 1] = self.mixB_state(nxt, self.psets[(n_ + 1) % 2])
                last_of_pair = (nxt is None) or (nxt[0] != it[0])
                g = self.mixB_chunks(i, it, states[n_], states.get(n_ + 1),
                                     flags[n_ + 1] if nxt is not None else None,
                                     last_of_pair and (4 in tg))
                if pending_epi is not None:
                    for _part in range(2):
                        next(g, _END)
                        next(pending_epi, None)
                    for _ in pending_epi:
                        pass
                late = self.flagged(prep(n_ + 1, "L"), flags[n_ + 1]) if nxt is not None else None
                if late is not None and states[n_]["sample"]:
                    for _ in late:
                        pass
                    late = None
                self.run_streams(g, self.chain(prep(1, "E") if n_ == 0 else None, late, prep(n_ + 2, "E")))
                pending_epi = self.mixB_epi(it, self.psets[n_ % 2], prm_)
                next(pending_epi)
            for _ in pending_epi:
                pass
            self.PJ = self.PJ2 + self.PX2
            self.cp("mixB")
            for g in range(4):
                W = self.wload(wdr, [128 * g, 512 + 128 * g, 1024 + 128 * g, 1536 + 128 * g], 128)
                lc = 0
                for tbi in tg:
                    self.mixerA_tb(g, tbi, W, lc, cw)
                    lc += TBS[tbi][1]
            self.cp("mixA")
            self.outproj(tg, d["w_out_even"][i])
            self.cp("tg_even")
        self.dma("sp", o["oconv"][i], self.oconv.re("p g s r -> p (g s r)"))
        self.dma("sp", o["oshift"][i], self.oshift.re("p c s -> p (c s)"))

    def mixerA_tb(self, g, tbi, W, lc, cw):
        g0, n, nseg, sl = TBS[tbi]
        f = self.f
        aH = f[2][:, 0:n]
        ps_h = self.proj(W, 0, 128, lc, n)
        self.copy("act", aH, ps_h[:, 0:n])
        mext = f[0][:, 0:nseg * (sl + 2)].re("p (s l) -> p s l", l=sl + 2)
        ps_c = self.proj(W, 2, 128, lc, n)
        self.tt("dve", mext[:, :, 2:sl + 2], ps_c[:, 0:n].re("p (s l) -> p s l", l=sl),
                aH.re("p (s l) -> p s l", l=sl), ALU.mult)
        if nseg == 1:
            self.copy("act", mext[:, 0, 0:2], self.cc[:, g, :])
        else:
            self.copy("act", mext[:, :, 0:2], self.sconv[:, g, :, :])
        ps_z = self.proj(W, 3, 128, lc, n)
        sz = f[3][:, 0:n]
        self.act(sz, ps_z[:, 0:n], AF.Silu)
        ps_b = self.proj(W, 1, 128, lc, n)
        gate = f[4][:, 0:n]
        self.tt("dve", gate, ps_b[:, 0:n], sz, ALU.mult)
        c0 = f[5][:, 0:n].re("p (s l) -> p s l", l=sl)
        c1 = f[1][:, 0:n].re("p (s l) -> p s l", l=sl)
        self.ts("dve", c0, mext[:, :, 2:sl + 2], cw[:, g, 2:3], ALU.mult)
        self.stt("dve", c1, mext[:, :, 1:sl + 1], cw[:, g, 1:2], c0, ALU.mult, ALU.add)
        self.stt("dve", c0, mext[:, :, 0:sl], cw[:, g, 0:1], c1, ALU.mult, ALU.add)
        self.tt("dve", self.yTa[:, g, lc:lc + n], f[5][:, 0:n], gate, ALU.mult)
        if nseg == 1:
            self.copy("act", self.cc[:, g, :], mext[:, 0, sl:sl + 2])
            if g0 + n == SEQ:
                self.copy("act", self.oconv[:, g, 0, :], mext[:, 0, sl:sl + 2])
        else:
            self.copy("act", self.oconv[:, g, 1:17, :], mext[:, :, sl:sl + 2])

    def mixB_prep(self, it, ps_, prm_, part):
        pb, tbi, W, lc = it
        mu, negw0, a0, kkw, ka, omka, rk, gng, gnb = prm_
        g0, n, nseg, sl = TBS[tbi]
        f, b = self.f, self.b
        sample = nseg > 1
        wsl = 8 if sample else 128
        nws = n // wsl
        pcol = slice(pb, pb + 1)
        rS, kS, vS = f[2][:, 0:n], f[3][:, 0:n], f[4][:, 0:n]
        tneg = f[5][:, 0:n]
        a = f[6][:, 0:n]
        kk = f[7][:, 0:n]
        km = f[9][:, 0:n]
        pa = f[10][:, 0:n]
        Tc = f[12][:, 0:n]
        rkb = b[3][:, 0:n]
        x1 = f[1][:, 0:n]
        eX = f[15][:, 0:n]
        eX3 = f[0][:, 0:n]
        if part == "E":
            self.shift_evac(self.proj(W, 0, 128, lc, n), tbi, pb, rS, mu)
            yield
            self.shift_evac(self.proj(W, 1, 128, lc, n), tbi, 4 + pb, kS, mu)
            yield
            self.shift_evac(self.proj(W, 2, 128, lc, n), tbi, 8 + pb, vS, mu)
            yield
            ps = self.pj()
            self.mm(ps[:, 0:n], self.lora2[0:64, pb * 128:(pb + 1) * 128], self.lora[0:64, lc:lc + n])
            self.act(tneg, ps[:, 0:n], AF.Exp, scale=-1.0, bias=negw0[:, pcol])
            self.act(tneg, tneg, AF.Ln, bias=1.0)
            self.act(tneg, tneg, AF.Exp, scale=-1.0, bias=-0.5)
            yield
            ps = self.pj()
            self.mm(ps[:, 0:n], self.lora2[64:128, pb * 128:(pb + 1) * 128], self.lora[64:128, lc:lc + n])
            self.act(a, ps[:, 0:n], AF.Exp, scale=-1.0, bias=a0[:, pcol])
            self.act(a, a, AF.Ln, bias=1.0)
            self.act(a, a, AF.Exp, scale=-1.0)
            self.ts("dve", kk, kS, kkw[:, pcol], ALU.mult)
            sqb = b[3][:, 0:n]
            self.tt("pool", sqb, kk, kk, ALU.mult)
            yield
            ps = self.pj()
            self.mm(ps[:, 0:n], self.blk64, sqb)
            nrm = f[0][:, 0:n]
            self.act(nrm, ps[:, 0:n], AF.Ln, bias=1e-24)
            self.act(nrm, nrm, AF.Exp, scale=-0.5)
            self.tt("dve", kk, kk, nrm, ALU.mult)
            yield
            self.ts("dve", km, a, ka[:, pcol], ALU.mult, omka[:, pcol], ALU.add)
            self.tt("dve", km, km, kS, ALU.mult)
            self.tt("pool", pa, kk, a, ALU.mult)
            self.stt("dve", rkb, rS, rk[:, pcol], km, ALU.mult, ALU.mult)
            self.scan(Tc, (self.rsts if sample else self.rstp)[:, 0:n], tneg)
            yield
            self.tt("dve", x1, Tc, tneg, ALU.subtract)
            self.act(eX, x1, AF.Exp, scale=-1.0)
            Tc3 = Tc.re("p (s l) -> p s l", l=wsl)
            self.tt("dve", x1.re("p (s l) -> p s l", l=wsl), Tc3[:, :, wsl - 1:wsl].bc([128, nws, wsl]), Tc3,
                    ALU.subtract)
            self.act(eX3, x1, AF.Exp, scale=-1.0)
            self.act(x1, Tc, AF.Exp)
            yield
            return
        eX2 = x1
        gz = ps_["gz"][:, 0:n]
        self.act(gz, self.proj(W, 3, 128, lc, n)[:, 0:n], AF.Silu)
        ps = self.pj()
        self.mm(ps[:, 0:n], self.blk64, rkb)
        bonus = ps_["bonus"][:, 0:n]
        self.tt("dve", bonus, ps[:, 0:n], vS, ALU.mult)
        eL = ps_["eL"][:, 0:n]
        self.act(eL, Tc, AF.Exp, scale=-1.0)
        yield
        QRF = ps_["QRF"]
        PF, KF, PhF, KhF, VF = (ps_[k][:, 0:n] for k in ("PF", "KF", "PhF", "KhF", "VF"))
        self.stt("dve", QRF[:, 0, 0:n], kk, -1.0, eX, ALU.mult, ALU.mult)
        self.tt("dve", QRF[:, 1, 0:n], rS, eL, ALU.mult)
        self.tt("pool", PF, pa, eX2, ALU.mult)
        self.tt("dve", KF, km, eX2, ALU.mult)
        self.tt("dve", PhF, pa, eX3, ALU.mult)
        self.tt("pool", KhF, km, eX3, ALU.mult)
        self.copy("act", VF, vS)
        yield

    def chain(self, *gens):
        for g in gens:
            if g is not None:
                for _ in g:
                    yield

    def merge(self, *gens):
        gens = [g for g in gens if g is not None]
        while gens:
            for g in list(gens):
                try:
                    next(g)
                except StopIteration:
                    gens.remove(g)
            yield

    def flagged(self, gen, flag):
        for _ in gen:
            yield
        flag[0] = True

    def mixB_state(self, it, ps_):
        pb, tbi, W, lc = it
        g0, n, nseg, sl = TBS[tbi]
        sample = nseg > 1
        var = 1 if sample else 0
        nch = n // 128
        sets = []
        for c in range(nch):
            self.chunk_ctr += 1
            sets.append(self.chunk_ctr % 4)
        PF, KF, PhF, KhF, VF = (ps_[k][:, 0:n] for k in ("PF", "KF", "PhF", "KhF", "VF"))
        eL = ps_["eL"][:, 0:n]
        p1 = [self.wkv_p1(sets[c], c, var, PF, KF, PhF, KhF, VF, ps_["QRF"]) for c in range(nch)]
        p2 = [self.wkv_p2(sets[c], pb, c, sample, ps_["QRF"], eL) for c in range(nch)]
        return dict(nch=nch, sample=sample, p1=p1, p2=p2, done1=set())

    def mixB_chunks(self, i, it, st, nst, nflag, last_of_pair):
        pb, tbi, W, lc = it
        sample = st["sample"]
        nch = st["nch"]
        d, o = self.d, self.o
        if sample:
            self.dma("sp", self.Hs, d["swkv"][i, pb].rearrange("p (b e) -> p b e", b=NS))
            self.copy("act", self.Hb, self.Hs)
        else:
            for hh in range(2):
                hp = slice(64 * hh, 64 * hh + 64)
                self.copy("act", self.Hbd[hp, hh * 64:(hh + 1) * 64], self.Hp[hp, pb, :])
        p1, p2, done1 = st["p1"], st["p2"], st["done1"]
        cur = 0
        look = 3 if sample else 4
        while cur < nch:
            if cur in done1:
                try:
                    next(p2[cur])
                except StopIteration:
                    cur += 1
                    continue
            nact = 0
            for idx in range(cur, cur + look):
                if nact >= 3:
                    break
                if idx < nch:
                    if idx not in done1:
                        nact += 1
                        try:
                            next(p1[idx])
                        except StopIteration:
                            done1.add(idx)
                elif nst is not None and nflag[0]:
                    j = idx - nch
                    if j < nst["nch"] and j not in nst["done1"]:
                        nact += 1
                        try:
                            next(nst["p1"][j])
                        except StopIteration:
                            nst["done1"].add(j)
            yield
        if last_of_pair:
            ov = o["owkv"][i, pb].rearrange("p (s e) -> p s e", s=17)
            self.dma("sp", ov[:, 0, :], self.Hp[:, pb, :])
            self.dma("sp", ov[:, 1:17, :], self.Hs)

    def mixB_epi(self, it, ps_, prm_):
        pb, tbi, W, lc = it
        mu, negw0, a0, kkw, ka, omka, rk, gng, gnb = prm_
        g0, n, nseg, sl = TBS[tbi]
        f, b = self.f, self.b
        nch = n // 128
        pcol = slice(pb, pb + 1)
        bonus = ps_["bonus"][:, 0:n]
        gz = ps_["gz"][:, 0:n]
        G = nch * 2
        Ysb = ps_["eL"][:, 0:n]
        Y3 = Ysb.re("p (g e) -> p g e", e=64)
        self.copy("act", Ysb, self.PY[:, 0:n])
        s1 = self.small[:, 0:G]
        self.rsum(s1, Y3)
        self.ts("dve", s1, s1, 1.0 / 64, ALU.mult)
        self.tt("dve", Y3, Y3, s1.un(2).bc([128, G, 64]), ALU.subtract)
        sq = ps_["sqf"][:, 0:n]
        self.tt("dve", sq, Ysb, Ysb, ALU.mult)
        s2 = self.small[:, 8:8 + G]
        self.rsum(s2, sq.re("p (g e) -> p g e", e=64))
        yield
        self.act(s2, s2, AF.Ln, scale=1.0 / 64, bias=64e-5)
        self.act(s2, s2, AF.Exp, scale=-0.5)
        Yn = b[0][:, 0:n]
        self.tt("dve", Yn.re("p (g e) -> p g e", e=64), Y3, s2.un(2).bc([128, G, 64]), ALU.mult)
        yield
        psT = self.pw().bitcast(BF16)
        for c in range(nch):
            self.tr(psT[:, c * 128:(c + 1) * 128], Yn[:, c * 128:(c + 1) * 128], self.identb)
        yb = Ysb
        self.ts("dve", yb, psT[:, 0:n], gng[:, pcol], ALU.mult, gnb[:, pcol], ALU.add)
        self.tt("dve", yb, yb, bonus, ALU.add)
        self.tt("dve", self.yTb[:, pb, lc:lc + n], yb, gz, ALU.mult)

    def run_streams(self, *gens):
        gens = [g for g in gens if g is not None]
        while gens:
            for g in list(gens):
                try:
                    next(g)
                except StopIteration:
                    gens.remove(g)

    def wkv_p1(self, st, c, var, PF, KF, PhF, KhF, VF, QRF):
        cs = slice(c * 128, (c + 1) * 128)
        tok3, AM = self.tok3s[st], self.AMs[st]
        psT = self.pw().bitcast(BF16)
        self.tr(psT[:, 0:128], VF[:, cs], self.identb)
        self.tr(psT[:, 128:256], PhF[:, cs], self.identb)
        self.tr(psT[:, 256:384], KhF[:, cs], self.identb)
        self.copy("act", tok3.re("p a t -> p (a t)"), psT[:, 0:384])
        for hh in range(2):
            hp = slice(64 * hh, 64 * hh + 64)
            bank = self.pw()
            self.mm(bank[:, 0:256].re("p (a t) -> p a t", a=2), PF[hp, cs], QRF[hp, :, cs])
            self.mm(bank[:, 256:512].re("p (a t) -> p a t", a=2), KF[hp, cs], QRF[hp, :, cs])
            self.tt("dve", AM[:, hh, :, :].re("p a t -> p (a t)"), bank, self.maskA[var], ALU.mult)
        yield
        psTn = self.pw().bitcast(BF16)
        for hh in range(2):
            self.tr(psTn[:, hh * 128:(hh + 1) * 128], AM[:, hh, 0, :], self.identb)
        Nping = self.Npings[st]
        Ncur = Nping[0]
        self.copy("act", Ncur.re("p h t -> p (h t)"), psTn[:, 0:256])
        BIG = self.BIGs[st]
        NT = BIG[:, 0, :, :]
        S = BIG[:, 1, :, :]
        self.tt("pool", S, AM[:, :, 0, :], self.identb.un(1).bc([128, 2, 128]), ALU.add)
        yield
        R = 3 if var else 7
        for r in range(1, R + 1):
            NTp = AM[:, :, 0, :] if r == 1 else NT
            Np = Nping[(r - 1) % 2]
            psa = self.pw()
            if r <= R - 2:
                for hh in range(2):
                    self.mm(psa[:, hh * 128:(hh + 1) * 128], Np[:, hh, :], NTp[:, hh, :])
            if r >= 2:
                for hh in range(2):
                    sx = psa[:, 256 + hh * 128:256 + (hh + 1) * 128]
                    self.mm(sx, self.identb, S[:, hh, :], start=True, stop=False)
                    self.mm(sx, Np[:, hh, :], S[:, hh, :], start=False, stop=True)
            if r <= R - 1:
                psb = self.pw()
                for hh in range(2):
                    self.mm(psb[:, hh * 128:(hh + 1) * 128], NTp[:, hh, :], Np[:, hh, :])
            if r == 1:
                self.copy("act", NT.re("p h t -> p (h t)"), psa[:, 0:256])
            elif r <= R - 2:
                self.copy("act", BIG.re("p a h t -> p (a h t)"), psa)
            else:
                self.copy("act", S.re("p h t -> p (h t)"), psa[:, 256:512])
            if r <= R - 1:
                self.copy("dve", Nping[r % 2].re("p h t -> p (h t)"), psb[:, 0:256])
            yield

    def wkv_p2(self, st, pb, c, sample, QRF, eL):
        nsg = NS if sample else 1
        wsl = 128 // nsg
        tok3, AM = self.tok3s[st], self.AMs[st]
        S7 = self.BIGs[st][:, 1, :, :]
        i64 = self.identb[0:64, 0:64]
        if not sample:
            cols = slice(c * 128, (c + 1) * 128)
            Hbd = self.Hbd
            psX = self.px()
            self.mm(psX[:, 0:128], QRF[:, 0, cols], Hbd, start=True, stop=False)
            for hh in range(2):
                self.mm(psX[:, hh * 64:(hh + 1) * 64], AM[:, hh, 2, :], tok3[:, 0, hh * 64:(hh + 1) * 64],
                        start=False, stop=(hh == 1))
            Xc = self.X[0]
            self.copy("act", Xc, psX[:, 0:128])
            yield
            ps = self.px()
            for hh in range(2):
                self.mm(ps[:, hh * 64:(hh + 1) * 64], S7[:, hh, :], Xc[:, hh * 64:(hh + 1) * 64])
            U = self.X[1]
            self.copy("act", U, ps[:, 0:128])
            yield
            self.mm(self.PY[:, c * 128:(c + 1) * 128], QRF[:, 1, cols], Hbd, start=True, stop=False)
            for hh in range(2):
                ys = self.PY[:, c * 128 + hh * 64:c * 128 + (hh + 1) * 64]
                self.mm(ys, AM[:, hh, 1, :], U[:, hh * 64:(hh + 1) * 64], start=False, stop=False)
                self.mm(ys, AM[:, hh, 3, :], tok3[:, 0, hh * 64:(hh + 1) * 64], start=False, stop=(hh == 1))
            psH = self.px()
            self.mm(psH[:, 0:128], tok3[:, 1, :], U, start=True, stop=False)
            self.mm(psH[:, 0:128], tok3[:, 2, :], tok3[:, 0, :], start=False, stop=True)
            for hh in range(2):
                hp = slice(64 * hh, 64 * hh + 64)
                self.stt("dve", Hbd[hp, hh * 64:(hh + 1) * 64], self.Hp[hp, pb, :],
                         eL[hp, c * 128 + 127:c * 128 + 128], psH[hp, hh * 64:(hh + 1) * 64], ALU.mult, ALU.add)
            for hh in range(2):
                hp = slice(64 * hh, 64 * hh + 64)
                hpb = self.Hp[hp, pb, :]
                self.stt("dve", hpb, hpb, eL[hp, c * 128 + 127:c * 128 + 128], psH[hp, hh * 64:(hh + 1) * 64],
                         ALU.mult, ALU.add)
            yield
            return
        for hh in range(2):
            hp = slice(64 * hh, 64 * hh + 64)
            psZ = self.px()
            pz = psZ[0:64, 0:256].re("p (a t) -> p a t", a=2)
            for sgi in range(nsg):
                cols = slice(c * 128 + sgi * wsl, c * 128 + (sgi + 1) * wsl)
                if nsg == 1:
                    self.mm(pz[:, :, sgi * wsl:(sgi + 1) * wsl], self.Hb[hp, sgi, :], QRF[hp, :, cols])
                else:
                    for a in range(2):
                        self.mm(pz[:, a, sgi * wsl:(sgi + 1) * wsl], self.Hb[hp, sgi, :], QRF[hp, a, cols])
            self.copy("act", self.ZT[:, hh, :, :].re("p a t -> p (a t)"), psZ[0:64, 0:256])
        yield
        psX = self.px()
        for hh in range(2):
            xs = psX[:, hh * 64:(hh + 1) * 64]
            self.mm(xs, self.ZT[:, hh, 0, :], i64, start=True, stop=False)
            self.mm(xs, AM[:, hh, 2, :], tok3[:, 0, hh * 64:(hh + 1) * 64], start=False, stop=True)
        Xc = self.X[0]
        self.copy("act", Xc, psX[:, 0:128])
        yield
        ps = self.px()
        for hh in range(2):
            self.mm(ps[:, hh * 64:(hh + 1) * 64], S7[:, hh, :], Xc[:, hh * 64:(hh + 1) * 64])
        Xn = self.X[1]
        self.copy("act", Xn, ps[:, 0:128])
        Xc = Xn
        yield
        U = Xc
        for hh in range(2):
            ys = self.PY[:, c * 128 + hh * 64:c * 128 + (hh + 1) * 64]
            self.mm(ys, self.ZT[:, hh, 1, :], i64, start=True, stop=False)
            self.mm(ys, AM[:, hh, 1, :], U[:, hh * 64:(hh + 1) * 64], start=False, stop=False)
            self.mm(ys, AM[:, hh, 3, :], tok3[:, 0, hh * 64:(hh + 1) * 64], start=False, stop=True)
        if not sample:
            psH = self.px()
            self.mm(psH[:, 0:128], tok3[:, 1, :], U, start=True, stop=False)
            self.mm(psH[:, 0:128], tok3[:, 2, :], tok3[:, 0, :], start=False, stop=True)
            for hh in range(2):
                hp = slice(64 * hh, 64 * hh + 64)
                hpb = self.Hp[hp, pb, :]
                self.stt("dve", hpb, hpb, eL[hp, c * 128 + 127:c * 128 + 128], psH[hp, hh * 64:(hh + 1) * 64],
                         ALU.mult, ALU.add)
            self.copy("act", self.Hb[:, 0, :], self.Hp[:, pb, :])
            yield
        else:
            eL3 = eL.re("p (s l) -> p s l", l=wsl)
            Um, Vm = self.UmVm[(st + 3) % 4]
            for q4 in range(4):
                sgs = slice(q4 * 4, q4 * 4 + 4)
                self.tt("dve", Um, U.un(1).bc([128, 4, 128]), self.segm[:, sgs].un(2).bc([128, 4, 128]),
                        ALU.mult)
                self.tt("dve", Vm, tok3[:, 0, :].un(1).bc([128, 4, 128]),
                        self.segm[:, sgs].un(2).bc([128, 4, 128]), ALU.mult)
                psH = self.px()
                for s4 in range(4):
                    hs = psH[:, s4 * 128:(s4 + 1) * 128]
                    self.mm(hs, tok3[:, 1, :], Um[:, s4, :], start=True, stop=False)
                    self.mm(hs, tok3[:, 2, :], Vm[:, s4, :], start=False, stop=True)
                p3 = psH.re("p (s e) -> p s e", s=4)
                for hh in range(2):
                    hp = slice(64 * hh, 64 * hh + 64)
                    hsb = self.Hs[hp, sgs, :]
                    self.tt("dve", hsb, hsb, eL3[hp, sgs, wsl - 1:wsl].bc([64, 4, 64]), ALU.mult)
                    self.tt("dve", hsb, hsb, p3[hp, :, hh * 64:(hh + 1) * 64], ALU.add)
                yield

    def odd_layer(self, i):
        d, o = self.d, self.o
        prm = self.prm
        f, b = self.f, self.b
        self.dma("sp", prm[:, 0:NPO], d["po"][i])
        normw = prm[:, PO_NORM:PO_NORM + 8]
        pscale = prm[:, PO_PSC:PO_PSC + 4]
        self.dma("sp", self.lnrow, d["lnrow"][i].partition_broadcast(128))
        lng = self.lnrow[:, 0:512]
        lnb = self.lnrow[:, 512:1024]
        spb = self.lnrow[:, 1024:1536].re("p (g t) -> p g t", g=4)
        self.dma("pool", self.poolw.re("p g d -> p (g d)"), d["poolw"][i])
        spT = f[0][:, 0:512]
        self.dma("sp", spT, d["spT"][i])
        maskI = self.maskA[0][:, 128:256]
        self.tt("dve", self.WsT, spT.re("p (g t) -> p g t", g=4), maskI.un(1).bc([128, 4, 128]), ALU.mult)
        ps = self.pj()
        self.mm(ps[:, 0:32].re("p (g t) -> p g t", g=4), self.rep8, spT[0:8, :].re("p (g t) -> p g t", g=4)[:, :, 0:8])
        rep = f[1][:, 0:32]
        self.copy("act", rep, ps[:, 0:32])
        maskIs = self.maskA[1][:, 128:256]
        for g in range(4):
            self.tt("dve", self.WsTs[:, g, :].re("p (b t) -> p b t", b=NS),
                    rep[:, g * 8:(g + 1) * 8].un(1).bc([128, NS, 8]),
                    maskIs.re("p (b t) -> p b t", b=NS), ALU.mult)
        self.memset("dve", self.pc, 0.0)
        wdr = d["w_in_odd"][i]
        for tg in TGS:
            self.norm(tg, normw)
            self.cp("o_norm")
            Wv = self.wload(wdr, [1536], 512)
            chunks = []
            lc = 0
            for tbi in tg:
                g0, n, nseg, sl = TBS[tbi]
                for c in range(n // 128):
                    chunks.append((lc + c * 128, nseg > 1))
                lc += n
            self.run_window([self.d1_chunk(i, cl, smp, Wv, ci % 5, lng, lnb, spb)
                             for ci, (cl, smp) in enumerate(chunks)], 5)
            self.cp("o_d1")
            for g in range(4):
                W = self.wload(wdr, [1024 + 128 * g, 2048 + 128 * g], 128)
                lc = 0
                for tbi in tg:
                    g0, n, nseg, sl = TBS[tbi]
                    ps_z = self.proj(W, 1, 128, lc, n)
                    sz = f[3][:, 0:n]
                    self.act(sz, ps_z[:, 0:n], AF.Silu)
                    ps_u = self.proj(W, 0, 128, lc, n)
                    t = f[4][:, 0:n]
                    self.tt("dve", t, ps_u[:, 0:n], self.mixT[:, g, lc:lc + n], ALU.mult)
                    self.tt("dve", self.yTb[:, g, lc:lc + n], t, sz, ALU.mult)
                    lc += n
            self.cp("o_d2")
            Wg = {}
            calls = []
            ci = 0
            for g in range(4):
                lc = 0
                for ti, tbi in enumerate(tg):
                    calls.append(self.mixerC_gen(i, g, tbi, lc, pscale, ci % 2, Wg, wdr, ti == 0, 4 in tg))
                    lc += TBS[tbi][1]
                    ci += 1
            self.run_window(calls, 2)
            self.cp("o_c")
            self.outproj(tg, d["w_out_odd"][i])
            self.cp("o_out")

    def run_window(self, gens, width):
        pending = list(gens)
        active = []
        while pending or active:
            while pending and len(active) < width:
                active.append(pending.pop(0))
            for g in list(active):
                try:
                    next(g)
                except StopIteration:
                    active.remove(g)

    def d1_sets(self):
        if not hasattr(self, "_d1sets"):
            f, b = self.f, self.b
            amf = [B(self.AMs[k].ap.rearrange("p h a t -> p (h a t)").bitcast(F32), "AM_%d" % k) for k in range(4)]
            qrff = B(self.QRF.ap.rearrange("p a t -> p (a t)").bitcast(F32), "QRF")
            hs = self.Hs.re("p b e -> p (b e)")
            self._d1sets = [
                (f[3][:, 0:512], f[4][:, 0:512], f[5][:, 0:512], b[0][:, 0:512]),
                (f[7][:, 0:512], amf[0], amf[1], b[2][:, 0:512]),
                (amf[2], hs[:, 0:512], hs[:, 512:1024], b[4][:, 0:512]),
                (f[6][:, 0:512], amf[3], qrff, b[5][:, 0:512]),
                (f[0][:, 0:512], f[1][:, 0:512], f[2][:, 0:512], b[6][:, 0:512]),
            ]
        return self._d1sets

    def d1_chunk(self, i, cl, sample, Wv, sset, lng, lnb, spb):
        o = self.o
        v, vc, vn, vnb = self.d1_sets()[sset]
        ss = self.small[:, 16 + 2 * sset:18 + 2 * sset]
        s1 = self.small[:, 16 + 2 * sset:17 + 2 * sset]
        s2 = self.small[:, 17 + 2 * sset:18 + 2 * sset]
        ps = self.pj()
        for k in range(8):
            self.mm(ps, self.hT[:, k, cl:cl + 128], Wv[:, k, :], start=(k == 0), stop=(k == 7))
        self.memset("dve", ss, 0.0)
        self.act(v, ps, AF.Identity, accum=s1)
        yield
        self.ts("dve", s1, s1, 1.0 / 512, ALU.mult)
        self.ts("dve", vc, v, s1, ALU.subtract)
        yield
        self.act(v, vc, AF.Square, accum=s2)
        self.act(s2, s2, AF.Ln, scale=1.0 / 512, bias=1e-5)
        self.act(s2, s2, AF.Exp, scale=-0.5)
        yield
        self.stt("dve", vn, vc, s2, lng, ALU.mult, ALU.mult)
        self.tt("dve", vn, vn, lnb, ALU.add)
        yield
        self.copy("act", vnb, vn)
        if sample:
            self.dma("sp", o["ochunkv"][i], vn)
        yield
        psM = self.pw()
        Ws = self.WsTs if sample else self.WsT
        for g in range(4):
            self.mm(psM[:, g * 128:(g + 1) * 128], vnb[:, g * 128:(g + 1) * 128], Ws[:, g, :])
        if sample:
            self.tt("dve", self.mixT[:, :, cl:cl + 128].re("p g (b t) -> p g b t", b=NS),
                    psM.re("p (g b t) -> p g b t", g=4, b=NS),
                    spb[:, :, 0:8].un(2).bc([128, 4, NS, 8]), ALU.add)
        else:
            self.tt("dve", self.mixT[:, :, cl:cl + 128], psM.re("p (g t) -> p g t", g=4), spb, ALU.add)

    def mixerC_gen(self, i, g, tbi, lc, pscale, sset, Wg, wdr, first, has_sample):
        g0, n, nseg, sl = TBS[tbi]
        f, b = self.f, self.b
        o, d = self.o, self.d
        if sset == 0:
            pextb, bufs, szb, plbb, tmpo = f[0], [f[1], f[2]], f[3], b[0], 32
        else:
            pextb, bufs, szb, plbb, tmpo = f[4], [f[5], f[6]], f[7], b[2], 48
        if first:
            Wg[g] = self.wload(wdr, [128 * g, 512 + 128 * g], 128)
            if has_sample:
                self.dma("sp", self.spool, d["spool"][i][:, g, :].rearrange("p (b r) -> p b r", b=NS))
        W = Wg[g]
        w = 2 << g
        L = sl + 15
        pext = pextb[:, 0:nseg * L].re("p (s l) -> p s l", l=L)
        ov = o["opool"][i, g].rearrange("p (s r) -> p s r", s=17)
        ps_p = self.proj(W, 0, 128, lc, n)
        self.copy("act", pext[:, :, 15:L], ps_p[:, 0:n].re("p (s l) -> p s l", l=sl))
        if nseg == 1:
            self.copy("dve", pext[:, 0, 0:15], self.pc[:, g, :])
            self.copy("act", self.pc[:, g, :], pext[:, 0, sl:sl + 15])
            if g0 + n == SEQ:
                self.dma("sp", ov[:, 0, :], pext[:, 0, sl:sl + 15])
        else:
            self.copy("dve", pext[:, :, 0:15], self.spool)
            self.dma("sp", ov[:, 1:17, :], pext[:, :, sl:sl + 15])
        ps_cz = self.proj(W, 1, 128, lc, n)
        sz = szb[:, 0:n]
        self.act(sz, ps_cz[:, 0:n], AF.Silu)
        yield
        cur = pext
        step = 1
        Lc = L
        for s_ in range(g + 1):
            nxt = bufs[s_ % 2][:, 0:nseg * L].re("p (s l) -> p s l", l=L)
            Ln = Lc - step
            self.tt("dve", nxt[:, :, 0:Ln], cur[:, :, step:Lc], cur[:, :, 0:Ln], ALU.add)
            cur, Lc, step = nxt, Ln, step * 2
        off = 16 - w
        win = cur[:, :, off:off + sl]
        plb = plbb[:, 0:n]
        self.stt("dve", plb.re("p (s l) -> p s l", l=sl), win, 1.0 / w, pext[:, :, 15:L], ALU.mult, ALU.subtract)
        if g0 == 0:
            tmp = self.small[:, tmpo:tmpo + w - 1]
            self.tt("dve", tmp, win[:, 0, 0:w - 1], self.invcnt[:, 0:w - 1], ALU.mult)
            self.tt("dve", plb[:, 0:w - 1], tmp, pext[:, 0, 15:15 + w - 1], ALU.subtract)
        yield
        ps_y = self.pj()
        self.mm(ps_y[:, 0:n], self.poolw[:, g, :], plb)
        self.stt("dve", self.yTa[:, g, lc:lc + n], ps_y[:, 0:n], pscale[:, g:g + 1], sz, ALU.mult, ALU.mult)


C_IDENT = 0
C_ONES = 128
C_BLK64 = 256
C_MASKA_P = 384
C_MASKA_S = 896
C_MASKN_P = 1408
C_MASKN_S = 1664
C_SEGM = 1920
CB_COLS = 1936
C_RSTP = 1936
C_RSTS = 2448
C_INVC = 2576
C_REP8 = 2592
CONST_COLS = 2720

PE_NORM, PE_CW, PE_MU, PE_W0, PE_A0, PE_KK, PE_KA, PE_RK, PE_GNG, PE_GNB = 0, 8, 20, 33, 37, 41, 45, 49, 53, 57
NPE = 61
PO_NORM, PO_PSC = 0, 8
NPO = 12


def make_consts():
    c = np.zeros((128, CONST_COLS), np.float32)
    idx = np.arange(128)
    c[:, C_IDENT:C_IDENT + 128] = np.eye(128)
    c[:, C_ONES:C_ONES + 128] = 1.0
    c[:, C_BLK64:C_BLK64 + 128] = (idx[:, None] // 64 == idx[None, :] // 64)
    s, t = idx[:, None], idx[None, :]
    for var, base_a, base_n in ((0, C_MASKA_P, C_MASKN_P), (1, C_MASKA_S, C_MASKN_S)):
        same = np.ones((128, 128), bool) if var == 0 else (s // 8 == t // 8)
        mts = (same & (s < t)).astype(np.float32)
        mti = (same & (s <= t)).astype(np.float32)
        mls = (same & (t < s)).astype(np.float32)
        for hh in range(2):
            c[:, base_a + hh * 256:base_a + hh * 256 + 128] = mts
            c[:, base_a + hh * 256 + 128:base_a + hh * 256 + 256] = mti
            c[:, base_n + hh * 128:base_n + (hh + 1) * 128] = mls
    c[:, C_SEGM:C_SEGM + NS] = (idx[:, None] // 8 == np.arange(NS)[None, :])
    rp = np.ones(512, np.float32)
    rp[::128] = 0.0
    c[:, C_RSTP:C_RSTP + 512] = rp[None, :]
    rs = np.ones(128, np.float32)
    rs[::8] = 0.0
    c[:, C_RSTS:C_RSTS + 128] = rs[None, :]
    c[:, C_INVC:C_INVC + 15] = (1.0 / np.arange(1, 16, dtype=np.float64)).astype(np.float32)[None, :]
    c[0:8, C_REP8:C_REP8 + 128] = (np.arange(8)[:, None] == (idx[None, :] % 8))
    return c


_CACHE = {}


def _fm(v, nchunk):
    return np.ascontiguousarray(np.asarray(v, np.float32).reshape(nchunk, 128).T)


def kernel(x_prompt, x_sample, state_conv, state_shift, state_wkv, state_pool,
           norm_w, final_norm_w, w_in_even, conv_w, shift_mu, w0, w2, a0, a2, k_k, k_a, r_k,
           gn_g, gn_b, w_out_even, w_in_odd, pool_w, pool_scale, v_ln_g, v_ln_b,
           spatial_w, spatial_b, w_out_odd):
    f32 = lambda a: np.ascontiguousarray(np.asarray(a, np.float32))
    x_prompt, x_sample = f32(x_prompt), f32(x_sample)
    state_conv, state_shift, state_wkv, state_pool = map(f32, (state_conv, state_shift, state_wkv, state_pool))
    if "nc" not in _CACHE:
        kb = K()
        stats = kb.build()
        _CACHE["nc"] = kb.nc
        _CACHE["stats"] = stats
    nc = _CACHE["nc"]
    pe = np.zeros((2, 128, NPE), np.float32)
    po = np.zeros((2, 128, NPO), np.float32)
    lora2 = np.zeros((2, 128, 512), np.float32)
    lnrow = np.zeros((2, 1536), np.float32)
    for i in range(2):
        pe[i, :, PE_NORM:PE_NORM + 8] = _fm(norm_w[2 * i], 8)
        cwm = np.asarray(conv_w[i], np.float32)
        pe[i, :, PE_CW:PE_CW + 12] = cwm.reshape(3, 4, 128).transpose(2, 1, 0).reshape(128, 12)
        pe[i, :, PE_MU:PE_MU + 13] = _fm(shift_mu[i], 13)
        for nm, arr in ((PE_W0, w0), (PE_A0, a0), (PE_KK, k_k), (PE_KA, k_a), (PE_GNG, gn_g), (PE_GNB, gn_b)):
            pe[i, :, nm:nm + 4] = _fm(arr[i], 4)
        pe[i, :, PE_RK:PE_RK + 4] = _fm(np.asarray(r_k[i]).reshape(512), 4)
        lora2[i, 0:64] = np.asarray(w2[i], np.float32)
        lora2[i, 64:128] = np.asarray(a2[i], np.float32)
        po[i, :, PO_NORM:PO_NORM + 8] = _fm(norm_w[2 * i + 1], 8)
        po[i, :, PO_PSC:PO_PSC + 4] = _fm(pool_scale[i], 4)
        lnrow[i, 0:512] = np.asarray(v_ln_g[i], np.float32)
        lnrow[i, 512:1024] = np.asarray(v_ln_b[i], np.float32)
        lnrow[i, 1024:1536] = np.asarray(spatial_b[i], np.float32).reshape(512)
    poolw = f32(np.asarray(pool_w, np.float32).transpose(0, 2, 1, 3).reshape(2, 128, 512))
    spT = f32(np.asarray(spatial_w, np.float32).transpose(0, 3, 1, 2).reshape(2, 128, 512))
    shared = {
        "w_in_even": f32(w_in_even), "w_out_even": f32(w_out_even),
        "w_in_odd": f32(w_in_odd), "w_out_odd": f32(w_out_odd),
        "consts": make_consts(), "pe": pe, "po": po, "lora2": lora2, "fnw": _fm(final_norm_w, 8),
        "lnrow": lnrow, "poolw": poolw, "spT": spT,
    }
    in_maps = []
    for c in range(NCORE):
        sl = slice(NS * c, NS * (c + 1))
        xt = np.concatenate([x_prompt[c].reshape(16, 128, D), x_sample[sl].reshape(1, 128, D)], axis=0)
        m = dict(shared)
        m["xtok"] = f32(xt)
        m["sconv"] = f32(state_conv[:, sl].reshape(2, NS, 2, 4, 128).transpose(0, 4, 3, 1, 2).reshape(2, 128, -1))
        m["sshift"] = f32(state_shift[:, sl].reshape(2, NS, 13, 128).transpose(0, 3, 2, 1).reshape(2, 128, -1))
        m["spool"] = f32(state_pool[:, sl].reshape(2, NS, 15, 4, 128).transpose(0, 4, 3, 1, 2).reshape(2, 128, 4, -1))
        m["swkv"] = f32(state_wkv[:, sl].reshape(2, NS, 4, 2, 64, 64).transpose(0, 2, 3, 5, 1, 4).reshape(2, 4, 128, -1))
        in_maps.append(m)
    res = run_bass_kernel_spmd(nc, in_maps, core_ids=list(range(NCORE)))
    R = res.results
    y_prompt = np.zeros((8, SEQ, D), np.float32)
    y_sample = np.zeros((128, ST, D), np.float32)
    conv_p = np.zeros((2, 8, 2, 512), np.float32)
    conv_s = np.zeros((2, 128, 2, 512), np.float32)
    shift_p = np.zeros((2, 8, 1664), np.float32)
    shift_s = np.zeros((2, 128, 1664), np.float32)
    wkv_p = np.zeros((2, 8, 8, 64, 64), np.float32)
    wkv_s = np.zeros((2, 128, 8, 64, 64), np.float32)
    pool_p = np.zeros((2, 8, 15, 512), np.float32)
    pool_s = np.zeros((2, 128, 15, 512), np.float32)
    chunkv = np.zeros((2, 128, ST, 512), np.float32)
    for c in range(NCORE):
        r = R[c]
        sl = slice(NS * c, NS * (c + 1))
        yt = np.asarray(r["ytok"])
        y_prompt[c] = yt[0:16].reshape(SEQ, D)
        y_sample[sl] = yt[16].reshape(NS, ST, D)
        oc = np.asarray(r["oconv"]).reshape(2, 128, 4, 17, 2)
        oc = oc.transpose(0, 3, 4, 2, 1).reshape(2, 17, 2, 512)
        conv_p[:, c] = oc[:, 0]
        conv_s[:, sl] = oc[:, 1:]
        osf = np.asarray(r["oshift"]).reshape(2, 128, 13, 17).transpose(0, 3, 2, 1).reshape(2, 17, 1664)
        shift_p[:, c] = osf[:, 0]
        shift_s[:, sl] = osf[:, 1:]
        ow = np.asarray(r["owkv"]).reshape(2, 4, 2, 64, 17, 64)
        ow = ow.transpose(0, 4, 1, 2, 5, 3).reshape(2, 17, 8, 64, 64)
        wkv_p[:, c] = ow[:, 0]
        wkv_s[:, sl] = ow[:, 1:]
        op = np.asarray(r["opool"]).reshape(2, 4, 128, 17, 15)
        op = op.transpose(0, 3, 4, 1, 2).reshape(2, 17, 15, 512)
        pool_p[:, c] = op[:, 0]
        pool_s[:, sl] = op[:, 1:]
        chunkv[:, sl] = np.asarray(r["ochunkv"]).reshape(2, NS, ST, 512)
    return (y_prompt, y_sample, conv_p, conv_s, shift_p, shift_s, wkv_p, wkv_s, pool_p, pool_s, chunkv)
```

```python
import numpy as np
import concourse.bass as bass
import concourse.mybir as mybir
from concourse.bass_utils import run_bass_kernel_spmd

F32 = mybir.dt.float32
BF16 = mybir.dt.bfloat16
AF = mybir.ActivationFunctionType
ALU = mybir.AluOpType
AX = mybir.AxisListType

NCORE = 8
D = 1024
SEQ = 2048
NS = 16
ST = 8
NTOK = SEQ + NS * ST
ECOLS = 4224
OCOLS = 2560
TBS = [(0, 512, 1, 512), (512, 512, 1, 512), (1024, 512, 1, 512), (1536, 512, 1, 512),
       (2048, 128, 16, 8)]
TGS = [[0, 1], [2, 3, 4]]
TGW = 1152
FW = 528


class _Op:
    __slots__ = ("eng", "fn", "reads", "writes", "is_dma", "deps", "signaled", "sem", "semval")

    def __init__(self, eng, fn, reads, writes, is_dma):
        self.eng = eng
        self.fn = fn
        self.reads = reads
        self.writes = writes
        self.is_dma = is_dma
        self.deps = []
        self.signaled = False
        self.sem = None
        self.semval = 0


class Prog:
    def __init__(self, nc, dma_pool=8):
        self.nc = nc
        self.ops = []
        self.eng_obj = {"pe": nc.tensor, "act": nc.scalar, "dve": nc.vector,
                        "pool": nc.gpsimd, "sp": nc.sync}
        self.dma_pool = dma_pool

    def op(self, eng, fn, reads=(), writes=()):
        reads = tuple(reads)
        writes = tuple(writes) + tuple(r for r in reads if r in PSUM_NAMES)
        self.ops.append(_Op(eng, fn, reads, writes, False))

    def dma(self, queue, fn, reads=(), writes=()):
        self.ops.append(_Op(queue, fn, tuple(reads), tuple(writes), True))

    def finalize(self):
        nc = self.nc
        ops = self.ops
        last_writer = {}
        readers = {}

        def need(prod, cons, kind):
            if prod.is_dma or cons.is_dma:
                return True
            if prod.eng != cons.eng:
                return True
            return prod.eng != "pe"

        dma_hist = {}
        for o in ops:
            deps = {}
            for b in o.reads:
                w = last_writer.get(b)
                if w is not None and need(w, o, "RAW"):
                    deps[id(w)] = w
            for b in o.writes:
                w = last_writer.get(b)
                if w is not None and need(w, o, "WAW"):
                    deps[id(w)] = w
                for r in readers.get(b, ()):
                    if r is not o and need(r, o, "WAR"):
                        deps[id(r)] = r
            if o.is_dma:
                h = dma_hist.setdefault(o.eng, [])
                if len(h) >= self.dma_pool:
                    p = h[len(h) - self.dma_pool]
                    deps[id(p)] = p
                h.append(o)
            for b in o.reads:
                readers.setdefault(b, []).append(o)
            for b in o.writes:
                last_writer[b] = o
                readers[b] = []
            o.deps = list(deps.values())
            for d in o.deps:
                d.signaled = True
        self.sems = {e: nc.alloc_semaphore("s_" + e) for e in self.eng_obj}
        dma_sems = {}
        cnt = {e: 0 for e in self.eng_obj}
        dcnt = {}
        dma_i = {}
        for o in ops:
            if o.is_dma:
                k = dma_i.get(o.eng, 0)
                dma_i[o.eng] = k + 1
                slot = (o.eng, k % self.dma_pool)
                if slot not in dma_sems:
                    dma_sems[slot] = nc.alloc_semaphore("d_%s_%d" % slot)
                dcnt[slot] = dcnt.get(slot, 0) + 16
                o.sem = dma_sems[slot]
                o.semval = dcnt[slot]
                o.signaled = True
            elif o.signaled:
                cnt[o.eng] += 1
                o.sem = self.sems[o.eng]
                o.semval = cnt[o.eng]
        waited = {e: {} for e in self.eng_obj}
        n_wait = 0
        for o in ops:
            eng = self.eng_obj[o.eng]
            wd = waited[o.eng]
            req = {}
            for d in o.deps:
                key = id(d.sem)
                if key not in req or req[key][1] < d.semval:
                    req[key] = (d.sem, d.semval)
            for key, (sem, val) in req.items():
                if wd.get(key, 0) >= val:
                    continue
                eng.wait_ge(sem, val)
                wd[key] = val
                n_wait += 1
            ins = o.fn(eng)
            if o.signaled:
                ins.then_inc(o.sem, 16 if o.is_dma else 1)
        for q, h in dma_hist.items():
            eng = self.eng_obj[q]
            last = {}
            for o in h:
                last[id(o.sem)] = o
            for o in last.values():
                eng.wait_ge(o.sem, o.semval)
        return len(ops), n_wait, max(cnt.values())


class B:
    __slots__ = ("ap", "names")

    def __init__(self, ap, names):
        self.ap = ap
        self.names = names if isinstance(names, tuple) else (names,)

    def __getitem__(self, k):
        return B(self.ap[k], self.names)

    def v(self, ap):
        return B(ap, self.names)

    def re(self, pat, **kw):
        return B(self.ap.rearrange(pat, **kw), self.names)

    def bc(self, shape):
        return B(self.ap.to_broadcast(list(shape)), self.names)

    def un(self, axis):
        return B(self.ap.unsqueeze(axis), self.names)

    def bitcast(self, dt):
        return B(self.ap.bitcast(dt), self.names)


def _nm(*xs):
    out = []
    for x in xs:
        if isinstance(x, B):
            out.extend(x.names)
    return out


def _ap(x):
    return x.ap if isinstance(x, B) else x


PSUM_NAMES = {"pj0", "px0", "px1", "pw0", "pw1", "pw2", "pw3", "py"}


class _Stop(Exception):
    pass


_END = object()


LIMIT = None


class K:
    def cp(self, name):
        if LIMIT is not None and name == LIMIT:
            raise _Stop()

    def __init__(self):
        self.nc = bass.Bass("TRN2", target_bir_lowering=False)
        self.P = Prog(self.nc)
        self.n_alloc = 0
        self.rr = 0
        self.pj_i = 0
        self.pw_i = 0
        self.wslot_i = 0
        self.chunk_ctr = 0

    def sb(self, name, shape, dt=F32):
        return B(self.nc.alloc_sbuf_tensor(name, list(shape), dt).ap(), name)

    def dram_in(self, name, shape, dt=F32):
        return self.nc.dram_tensor(name, list(shape), dt, kind="ExternalInput").ap()

    def dram_out(self, name, shape, dt=F32):
        return self.nc.dram_tensor(name, list(shape), dt, kind="ExternalOutput").ap()

    def tt(self, eng, out, a, b, op):
        self.P.op(eng, lambda e: e.tensor_tensor(out=out.ap, in0=a.ap, in1=b.ap, op=op),
                  reads=_nm(a, b), writes=_nm(out))

    def ts(self, eng, out, a, s1, op0, s2=None, op1=None):
        if op1 is None:
            self.P.op(eng, lambda e: e.tensor_scalar(out=out.ap, in0=a.ap, scalar1=_ap(s1), scalar2=None, op0=op0),
                      reads=_nm(a, s1), writes=_nm(out))
        else:
            self.P.op(eng, lambda e: e.tensor_scalar(out=out.ap, in0=a.ap, scalar1=_ap(s1), scalar2=_ap(s2),
                                                     op0=op0, op1=op1),
                      reads=_nm(a, s1, s2), writes=_nm(out))

    def stt(self, eng, out, a, s, b, op0, op1):
        self.P.op(eng, lambda e: e.scalar_tensor_tensor(out=out.ap, in0=a.ap, scalar=_ap(s), in1=b.ap,
                                                        op0=op0, op1=op1),
                  reads=_nm(a, s, b), writes=_nm(out))

    def act(self, out, a, func, scale=1.0, bias=0.0, accum=None):
        if accum is None:
            self.P.op("act", lambda e: e.activation(out=out.ap, in_=a.ap, func=func, bias=_ap(bias), scale=_ap(scale)),
                      reads=_nm(a, scale, bias), writes=_nm(out))
        else:
            self.P.op("act", lambda e: e.activation(out=out.ap, in_=a.ap, func=func, bias=_ap(bias), scale=_ap(scale),
                                                    accum_out=accum.ap),
                      reads=_nm(a, scale, bias), writes=_nm(out, accum))

    def copy(self, eng, out, a):
        if eng == "act":
            self.P.op("act", lambda e: e.copy(out=out.ap, in_=a.ap), reads=_nm(a), writes=_nm(out))
        else:
            self.P.op(eng, lambda e: e.tensor_copy(out=out.ap, in_=a.ap), reads=_nm(a), writes=_nm(out))

    def anycopy(self, out, a):
        self.rr += 1
        self.copy("act" if self.rr % 2 else "dve", out, a)

    def memset(self, eng, out, val):
        self.P.op(eng, lambda e: e.memset(out.ap, val), writes=_nm(out))

    def recip(self, out, a):
        self.P.op("dve", lambda e: e.reciprocal(out=out.ap, in_=a.ap), reads=_nm(a), writes=_nm(out))

    def rsum(self, out, a):
        self.P.op("dve", lambda e: e.reduce_sum(out=out.ap, in_=a.ap, axis=AX.X), reads=_nm(a), writes=_nm(out))

    def scan(self, out, d0, d1):
        self.P.op("dve", lambda e: e.tensor_tensor_scan(out=out.ap, data0=d0.ap, data1=d1.ap, initial=0.0,
                                                        op0=ALU.mult, op1=ALU.add),
                  reads=_nm(d0, d1), writes=_nm(out))

    def mm(self, out, lhsT, rhs, start=True, stop=True):
        self.P.op("pe", lambda e: e.matmul(out.ap, lhsT=lhsT.ap, rhs=rhs.ap, start=start, stop=stop),
                  reads=_nm(lhsT, rhs), writes=_nm(out))

    def tr(self, out, a, ident):
        self.P.op("pe", lambda e: e.transpose(out=out.ap, in_=a.ap, identity=ident.ap),
                  reads=_nm(a, ident), writes=_nm(out))

    def dma(self, q, out, a, slow=False):
        if slow:
            self.P.dma(q, lambda e: e.dma_start(out=_ap(out), in_=_ap(a), allow_slow_non_contiguous=True),
                       reads=_nm(a), writes=_nm(out))
        else:
            self.P.dma(q, lambda e: e.dma_start(out=_ap(out), in_=_ap(a)), reads=_nm(a), writes=_nm(out))

    def pj(self):
        self.pj_i += 1
        return self.PJ[self.pj_i % len(self.PJ)]

    def px(self):
        self.px_i += 1
        return self.PX[self.px_i % len(self.PX)]

    def pw(self):
        self.pw_i += 1
        return self.PW[self.pw_i % len(self.PW)]

    def wload(self, wdram, col_blocks, width):
        self.wslot_i += 1
        slot = self.WS[self.wslot_i % len(self.WS)]
        wv = wdram.rearrange("(k p) n -> p k n", p=128)
        for j, c0 in enumerate(col_blocks):
            self.dma("pool", slot[:, :, j * width:(j + 1) * width], wv[:, :, c0:c0 + width])
        return slot

    def build(self):
        nc = self.nc
        d = {}
        d["xtok"] = self.dram_in("xtok", [17, 128, D])
        d["w_in_even"] = self.dram_in("w_in_even", [2, D, ECOLS])
        d["w_out_even"] = self.dram_in("w_out_even", [2, D, D])
        d["w_in_odd"] = self.dram_in("w_in_odd", [2, D, OCOLS])
        d["w_out_odd"] = self.dram_in("w_out_odd", [2, D, D])
        d["consts"] = self.dram_in("consts", [128, CONST_COLS])
        d["pe"] = self.dram_in("pe", [2, 128, NPE])
        d["po"] = self.dram_in("po", [2, 128, NPO])
        d["lora2"] = self.dram_in("lora2", [2, 128, 512])
        d["fnw"] = self.dram_in("fnw", [128, 8])
        d["sconv"] = self.dram_in("sconv", [2, 128, 4 * NS * 2])
        d["sshift"] = self.dram_in("sshift", [2, 128, 13 * NS])
        d["spool"] = self.dram_in("spool", [2, 128, 4, NS * 15])
        d["swkv"] = self.dram_in("swkv", [2, 4, 128, NS * 64])
        d["lnrow"] = self.dram_in("lnrow", [2, 2 * 512 + 512])
        d["poolw"] = self.dram_in("poolw", [2, 128, 4 * 128])
        d["spT"] = self.dram_in("spT", [2, 128, 4 * 128])
        o = {}
        o["ytok"] = self.dram_out("ytok", [17, 128, D])
        o["oconv"] = self.dram_out("oconv", [2, 128, 4 * 17 * 2])
        o["oshift"] = self.dram_out("oshift", [2, 128, 13 * 17])
        o["owkv"] = self.dram_out("owkv", [2, 4, 128, 17 * 64])
        o["opool"] = self.dram_out("opool", [2, 4, 128, 17 * 15])
        o["ochunkv"] = self.dram_out("ochunkv", [2, 128, 512])
        self.d, self.o = d, o

        self.xF = self.sb("xF", [128, 8, NTOK])
        self.hT = self.sb("hT", [128, 8, TGW], BF16)
        self.yTa = self.sb("yTa", [128, 4, TGW], BF16)
        self.yTb = self.sb("yTb", [128, 4, TGW], BF16)
        self.ysrc = [self.yTa[:, k, :] for k in range(4)] + [self.yTb[:, k, :] for k in range(4)]
        self.WS = [self.sb("ws%d" % i, [128, 8, 512], BF16) for i in range(2)]
        self.cst = self.sb("cst", [128, 128 + CONST_COLS - CB_COLS])
        fbig = self.nc.alloc_sbuf_tensor("fbig", [128, 16 * FW], F32).ap()
        self.f = [B(fbig[:, i * FW:(i + 1) * FW], "f%d" % i) for i in range(16)]
        self.mixT = B(fbig[:, 8 * FW:8 * FW + 4 * TGW // 2].bitcast(BF16).rearrange("p (g t) -> p g t", g=4),
                      ("f8", "f9", "f10", "f11", "f12"))
        self.lnrow = B(fbig[:, 13 * FW:13 * FW + 1536], ("f13", "f14", "f15"))
        self.b = [self.sb("b%d" % i, [128, FW], BF16) if i != 1 else None for i in range(9)]
        self.QRF = self.sb("QRF", [128, 2, 512], BF16)
        ya = self.yTa.ap.rearrange("p g t -> p (g t)")

        def yab(o0, w):
            return B(ya[:, o0:o0 + w], "yTa")
        self.psets = [
            dict(QRF=self.QRF, PF=self.b[4], KF=self.b[5], PhF=self.b[6], KhF=self.b[7], VF=self.b[8],
                 gz=self.b[2], eL=self.f[13], bonus=self.f[11],
                 sqf=B(self.QRF.ap.rearrange("p a t -> p (a t)").bitcast(F32), "QRF")),
            dict(QRF=B(ya[:, 0:1024].rearrange("p (a t) -> p a t", a=2), "yTa"),
                 PF=yab(1024, FW), KF=yab(1024 + FW, FW), PhF=yab(1024 + 2 * FW, FW), KhF=yab(1024 + 3 * FW, FW),
                 VF=yab(1024 + 4 * FW, FW), gz=yab(1024 + 5 * FW, FW), eL=self.f[8], bonus=self.f[14],
                 sqf=B(ya[:, 0:1024].bitcast(F32), "yTa")),
        ]
        self.lora = self.sb("lora", [128, TGW], BF16)
        self.lora2 = self.sb("lora2_s", [128, 512], BF16)
        self.prm = self.sb("prm", [128, max(NPE, NPO) + 16])
        self.fnw = self.sb("fnw_s", [128, 8])
        self.poolw = self.sb("poolw_s", [128, 4, 128], BF16)
        self.WsT = self.sb("WsT", [128, 4, 128], BF16)
        self.WsTs = self.sb("WsTs", [128, 4, 128], BF16)
        self.cc = self.sb("cc", [128, 4, 2])
        self.sc = self.sb("sc", [128, 13])
        self.pc = self.sb("pc", [128, 4, 15])
        self.Hp = self.sb("Hp", [128, 4, 64])
        self.Hs = self.sb("Hs", [128, NS, 64])
        self.Hb = self.sb("Hb", [128, NS, 64], BF16)
        self.Hbd = self.sb("Hbd", [128, 128], BF16)
        self.sconv = self.sb("sconv_s", [128, 4, NS, 2])
        self.sshift = self.sb("sshift_s", [128, 13, NS])
        self.spool = self.sb("spool_s", [128, NS, 15])
        self.oconv = self.sb("oconv_s", [128, 4, 17, 2])
        self.oshift = self.sb("oshift_s", [128, 13, 17])
        self.small = self.sb("small", [128, 64])
        self.tok3s = [self.sb("tok3_%d" % i, [128, 3, 128], BF16) for i in range(3)]
        self.AMs = [self.sb("AM_%d" % i, [128, 2, 4, 128], BF16) for i in range(4)]
        self.tok3s.append(B(self.spool.ap.rearrange("p b r -> p (b r)").bitcast(BF16)[:, 0:384]
                            .rearrange("p (a t) -> p a t", a=3), "spool_s"))
        self.BIGs = [self.sb("NS%d" % st, [128, 2, 2, 128], BF16) for st in range(4)]
        self.Npings = [[self.sb("Nping%d_%d" % (st, i), [128, 2, 128], BF16) for i in range(2)] for st in range(3)]
        self.Npings.append([self.WsT[:, 0:2, :], self.WsT[:, 2:4, :]])
        self.ZT = self.sb("ZT", [64, 2, 2, 128], BF16)
        self.X = [self.sb("X%d" % i, [128, 128], BF16) for i in range(2)]
        self.UmVm = []
        for si in range(4):
            am = self.AMs[si].ap.rearrange("p h a t -> p (h a t)")
            self.UmVm.append((B(am[:, 0:512].rearrange("p (s t) -> p s t", s=4), "AM_%d" % si),
                              B(am[:, 512:1024].rearrange("p (s t) -> p s t", s=4), "AM_%d" % si)))
        self.cb = self.sb("cb", [128, CB_COLS], BF16)
        self.PJ2 = [B(nc.alloc_psum_tensor("pj%d" % i, [128, 512], F32).ap(), "pj%d" % i) for i in range(1)]
        self.PX2 = [B(nc.alloc_psum_tensor("px%d" % i, [128, 512], F32).ap(), "px%d" % i) for i in range(2)]
        self.PW = [B(nc.alloc_psum_tensor("pw%d" % i, [128, 512], F32).ap(), "pw%d" % i) for i in range(4)]
        self.PJ = self.PJ2 + self.PX2
        self.PX = self.PX2
        self.px_i = 0
        self.PY = B(nc.alloc_psum_tensor("py", [128, 512], F32).ap(), "py")

        self.dma("sp", self.cst[:, 0:128], d["consts"][:, 0:128])
        self.dma("sp", self.cst[:, 128:], d["consts"][:, CB_COLS:CONST_COLS])
        self.dma("pool", self.cb, d["consts"][:, 0:CB_COLS])
        self.dma("sp", self.fnw, d["fnw"])
        c = self.cst
        FO = 128 - CB_COLS
        self.identf = c[:, 0:128]
        self.rstp = c[:, FO + C_RSTP:FO + C_RSTP + 512]
        self.rsts = c[:, FO + C_RSTS:FO + C_RSTS + 128]
        self.invcnt = c[:, FO + C_INVC:FO + C_INVC + 15]
        cb = self.cb
        self.identb = cb[:, C_IDENT:C_IDENT + 128]
        self.onesb = cb[:, C_ONES:C_ONES + 128]
        self.blk64 = cb[:, C_BLK64:C_BLK64 + 128]
        self.maskA = [cb[:, C_MASKA_P:C_MASKA_P + 512], cb[:, C_MASKA_S:C_MASKA_S + 512]]
        self.maskN = [cb[:, C_MASKN_P:C_MASKN_P + 256], cb[:, C_MASKN_S:C_MASKN_S + 256]]
        self.segm = cb[:, C_SEGM:C_SEGM + NS]
        self.rep8 = c[0:8, FO + C_REP8:FO + C_REP8 + 128]

        self.memset("dve", self.Hbd, 0.0)
        st0 = B(fbig[:, 0:1024], ("f0", "f1"))
        st1 = B(fbig[:, 2 * FW:2 * FW + 1024], ("f2", "f3"))
        stages = [st0, st1]
        for j in range(17):
            st = stages[j % 2]
            self.dma("sp", st, d["xtok"][j])
            for half in range(2):
                ps = self.pj()
                for q in range(4):
                    k = half * 4 + q
                    self.mm(ps[:, q * 128:(q + 1) * 128], st[:, k * 128:(k + 1) * 128], self.identf)
                self.anycopy(self.xF[:, half * 4:half * 4 + 4, j * 128:(j + 1) * 128],
                             ps.re("p (q t) -> p q t", q=4))

        try:
            self.cp("xload")
            for layer in range(4):
                if layer % 2 == 0:
                    self.even_layer(layer // 2)
                else:
                    self.odd_layer(layer // 2)
                self.cp("layer%d" % layer)
        except _Stop:
            pass

        for tg in TGS:
            for tbi in tg:
                g0, n, nseg, sl = TBS[tbi]
                rstd = self.norm_rstd(g0, n)
                for c0 in range(0, n, 128):
                    j = (g0 + c0) // 128
                    yfv = (B(fbig[:, 4 * FW:4 * FW + 1024].rearrange("p (k t) -> p k t", k=8), ("f4", "f5")),
                           B(fbig[:, 8 * FW:8 * FW + 1024].rearrange("p (k t) -> p k t", k=8), ("f8", "f9")))[j % 2]
                    for k in range(8):
                        self.stt("dve", yfv[:, k, :], self.xF[:, k, g0 + c0:g0 + c0 + 128], self.fnw[:, k:k + 1],
                                 rstd[:, c0:c0 + 128], ALU.mult, ALU.mult)
                    st = stages[j % 2]
                    for half in range(2):
                        ps = self.pj()
                        for q in range(4):
                            k = half * 4 + q
                            self.mm(ps[:, q * 128:(q + 1) * 128], yfv[:, k, :], self.identf)
                        self.copy("act", st[:, half * 512:(half + 1) * 512], ps)
                    self.dma("sp", o["ytok"][j], st)
        return self.P.finalize()

    def norm_rstd(self, g0, n):
        ps = self.pj()
        for k in range(8):
            sq = self.b[(0, 3)[k % 2]][:, 0:n]
            self.act(sq, self.xF[:, k, g0:g0 + n], AF.Square)
            self.mm(ps[:, 0:n], self.onesb, sq, start=(k == 0), stop=(k == 7))
        rstd = self.f[6][:, 0:n]
        self.act(rstd, ps[:, 0:n], AF.Ln, scale=1.0 / D, bias=1e-6)
        self.act(rstd, rstd, AF.Exp, scale=-0.5)
        return rstd

    def norm(self, tg, gcol):
        lc = 0
        for tbi in tg:
            g0, n, nseg, sl = TBS[tbi]
            rstd = self.norm_rstd(g0, n)
            for k in range(8):
                self.stt("dve", self.hT[:, k, lc:lc + n], self.xF[:, k, g0:g0 + n], gcol[:, k:k + 1], rstd,
                         ALU.mult, ALU.mult)
            lc += n

    def proj(self, W, j, width, lc, n, src=None):
        ps = self.pj()
        for k in range(8):
            sk = self.hT[:, k, lc:lc + n] if src is None else src[k][:, lc:lc + n]
            self.mm(ps[:, 0:n], W[:, k, j * width:(j + 1) * width], sk,
                    start=(k == 0), stop=(k == 7))
        return ps

    def outproj(self, tg, wdram):
        for dblk in range(8):
            W = self.wload(wdram, [dblk * 128], 128)
            lc = 0
            for tbi in tg:
                g0, n, nseg, sl = TBS[tbi]
                ps = self.proj(W, 0, 128, lc, n, src=self.ysrc)
                xs = self.xF[:, dblk, g0:g0 + n]
                self.tt("dve", xs, xs, ps[:, 0:n], ALU.add)
                lc += n

    def shift_evac(self, ps, tbi, chunk, dst, mu):
        g0, n, nseg, sl = TBS[tbi]
        qraw = self.f[0]
        dtmp = self.f[1]
        q3 = qraw[:, 0:nseg * (sl + 1)].re("p (s l) -> p s l", l=sl + 1)
        self.copy("act", q3[:, :, 1:sl + 1], ps[:, 0:n].re("p (s l) -> p s l", l=sl))
        if nseg == 1:
            self.copy("dve", q3[:, 0, 0:1], self.sc[:, chunk:chunk + 1])
        else:
            self.copy("dve", q3[:, :, 0], self.sshift[:, chunk, :])
        d3 = dtmp[:, 0:n].re("p (s l) -> p s l", l=sl)
        self.tt("dve", d3, q3[:, :, 0:sl], q3[:, :, 1:sl + 1], ALU.subtract)
        self.stt("dve", dst[:, 0:n].re("p (s l) -> p s l", l=sl), d3, mu[:, chunk:chunk + 1], q3[:, :, 1:sl + 1],
                 ALU.mult, ALU.add)
        if nseg == 1:
            self.copy("act", self.sc[:, chunk:chunk + 1], q3[:, 0, sl:sl + 1])
            if g0 + n == SEQ:
                self.copy("act", self.oshift[:, chunk, 0:1], q3[:, 0, sl:sl + 1])
        else:
            self.copy("act", self.oshift[:, chunk, 1:17], q3[:, :, sl])

    def even_layer(self, i):
        d, o = self.d, self.o
        prm = self.prm
        self.dma("sp", prm[:, 0:NPE], d["pe"][i])
        self.dma("pool", self.lora2, d["lora2"][i])
        self.dma("sp", self.sconv, d["sconv"][i].rearrange("p (g b r) -> p g b r", g=4, b=NS))
        self.dma("sp", self.sshift, d["sshift"][i].rearrange("p (c b) -> p c b", c=13))
        normw = prm[:, PE_NORM:PE_NORM + 8]
        cw = prm[:, PE_CW:PE_CW + 12].re("p (g j) -> p g j", g=4)
        mu = prm[:, PE_MU:PE_MU + 13]
        w0 = prm[:, PE_W0:PE_W0 + 4]
        a0 = prm[:, PE_A0:PE_A0 + 4]
        kkw = prm[:, PE_KK:PE_KK + 4]
        ka = prm[:, PE_KA:PE_KA + 4]
        rk = prm[:, PE_RK:PE_RK + 4]
        gng = prm[:, PE_GNG:PE_GNG + 4]
        gnb = prm[:, PE_GNB:PE_GNB + 4]
        base = max(NPE, NPO)
        negw0 = prm[:, base:base + 4]
        omka = prm[:, base + 4:base + 8]
        self.ts("dve", negw0, w0, -1.0, ALU.mult)
        nega0 = prm[:, base + 8:base + 12]
        self.ts("dve", nega0, a0, -1.0, ALU.mult)
        a0 = nega0
        self.ts("dve", omka, ka, -1.0, ALU.mult, 1.0, ALU.add)
        self.memset("dve", self.cc, 0.0)
        self.memset("dve", self.sc, 0.0)
        self.memset("dve", self.Hp, 0.0)
        wdr = d["w_in_even"][i]
        for tg in TGS:
            self.norm(tg, normw)
            self.cp("norm")
            W = self.wload(wdr, [3584], 128)
            lc = 0
            for tbi in tg:
                g0, n, nseg, sl = TBS[tbi]
                ps = self.proj(W, 0, 128, lc, n)
                sh = self.f[2]
                self.shift_evac(ps, tbi, 12, sh, mu)
                self.act(self.lora[0:64, lc:lc + n], sh[0:64, 0:n], AF.Tanh)
                self.copy("dve", self.lora[64:128, lc:lc + n], sh[64:128, 0:n])
                lc += n
            self.cp("lora")
            prm_ = (mu, negw0, a0, kkw, ka, omka, rk, gng, gnb)
            self.PJ, self.PX = self.PJ2, self.PX2
            items = []
            for pb in range(4):
                lc = 0
                for tbi in tg:
                    items.append([pb, tbi, None, lc])
                    lc += TBS[tbi][1]

            def wl(pb):
                return self.wload(wdr, [2048 + 128 * pb, 2560 + 128 * pb, 3072 + 128 * pb, 3712 + 128 * pb], 128)

            Wcur = {}

            def getW(k):
                if k < len(items):
                    pbk = items[k][0]
                    if pbk not in Wcur:
                        Wcur[pbk] = wl(pbk)
                    items[k][2] = Wcur[pbk]

            def prep(k, part):
                if k >= len(items):
                    return None
                return self.mixB_prep(items[k], self.psets[k % 2], prm_, part)

            getW(0)
            getW(1)
            self.run_streams(self.chain(prep(0, "E"), prep(0, "L")))
            flags = [[False] for _ in items]
            flags[0][0] = True
            states = {0: self.mixB_state(items[0], self.psets[0])}
            pending_epi = None
            for n_ in range(len(items)):
                it = items[n_]
                nxt = items[n_ + 1] if n_ + 1 < len(items) else None
                getW(n_ + 2)
                if nxt is not None:
                    states[n_ + 1] = self.mixB_state(nxt, self.psets[(n_ + 1) % 2])
                last_of_pair = (nxt is None) or (nxt[0] != it[0])
                g = self.mixB_chunks(i, it, states[n_], states.get(n_ + 1),
                                     flags[n_ + 1] if nxt is not None else None,
                                     last_of_pair and (4 in tg))
                if pending_epi is not None:
                    for _part in range(2):
                        next(g, _END)
                        next(pending_epi, None)
                    for _ in pending_epi:
                        pass
                late = self.flagged(prep(n_ + 1, "L"), flags[n_ + 1]) if nxt is not None else None
                if late is not None and states[n_]["sample"]:
                    for _ in late:
                        pass
                    late = None
                self.run_streams(g, self.chain(prep(1, "E") if n_ == 0 else None, late, prep(n_ + 2, "E")))
                pending_epi = self.mixB_epi(it, self.psets[n_ % 2], prm_)
                next(pending_epi)
            for _ in pending_epi:
                pass
            self.PJ = self.PJ2 + self.PX2
            self.cp("mixB")
            for g in range(4):
                W = self.wload(wdr, [128 * g, 512 + 128 * g, 1024 + 128 * g, 1536 + 128 * g], 128)
                lc = 0
                for tbi in tg:
                    self.mixerA_tb(g, tbi, W, lc, cw)
                    lc += TBS[tbi][1]
            self.cp("mixA")
            self.outproj(tg, d["w_out_even"][i])
            self.cp("tg_even")
        self.dma("sp", o["oconv"][i], self.oconv.re("p g s r -> p (g s r)"))
        self.dma("sp", o["oshift"][i], self.oshift.re("p c s -> p (c s)"))

    def mixerA_tb(self, g, tbi, W, lc, cw):
        g0, n, nseg, sl = TBS[tbi]
        f = self.f
        aH = f[2][:, 0:n]
        ps_h = self.proj(W, 0, 128, lc, n)
        self.copy("act", aH, ps_h[:, 0:n])
        mext = f[0][:, 0:nseg * (sl + 2)].re("p (s l) -> p s l", l=sl + 2)
        ps_c = self.proj(W, 2, 128, lc, n)
        self.tt("dve", mext[:, :, 2:sl + 2], ps_c[:, 0:n].re("p (s l) -> p s l", l=sl),
                aH.re("p (s l) -> p s l", l=sl), ALU.mult)
        if nseg == 1:
            self.copy("act", mext[:, 0, 0:2], self.cc[:, g, :])
        else:
            self.copy("act", mext[:, :, 0:2], self.sconv[:, g, :, :])
        ps_z = self.proj(W, 3, 128, lc, n)
        sz = f[3][:, 0:n]
        self.act(sz, ps_z[:, 0:n], AF.Silu)
        ps_b = self.proj(W, 1, 128, lc, n)
        gate = f[4][:, 0:n]
        self.tt("dve", gate, ps_b[:, 0:n], sz, ALU.mult)
        c0 = f[5][:, 0:n].re("p (s l) -> p s l", l=sl)
        c1 = f[1][:, 0:n].re("p (s l) -> p s l", l=sl)
        self.ts("dve", c0, mext[:, :, 2:sl + 2], cw[:, g, 2:3], ALU.mult)
        self.stt("dve", c1, mext[:, :, 1:sl + 1], cw[:, g, 1:2], c0, ALU.mult, ALU.add)
        self.stt("dve", c0, mext[:, :, 0:sl], cw[:, g, 0:1], c1, ALU.mult, ALU.add)
        self.tt("dve", self.yTa[:, g, lc:lc + n], f[5][:, 0:n], gate, ALU.mult)
        if nseg == 1:
            self.copy("act", self.cc[:, g, :], mext[:, 0, sl:sl + 2])
            if g0 + n == SEQ:
                self.copy("act", self.oconv[:, g, 0, :], mext[:, 0, sl:sl + 2])
        else:
            self.copy("act", self.oconv[:, g, 1:17, :], mext[:, :, sl:sl + 2])

    def mixB_prep(self, it, ps_, prm_, part):
        pb, tbi, W, lc = it
        mu, negw0, a0, kkw, ka, omka, rk, gng, gnb = prm_
        g0, n, nseg, sl = TBS[tbi]
        f, b = self.f, self.b
        sample = nseg > 1
        wsl = 8 if sample else 128
        nws = n // wsl
        pcol = slice(pb, pb + 1)
        rS, kS, vS = f[2][:, 0:n], f[3][:, 0:n], f[4][:, 0:n]
        tneg = f[5][:, 0:n]
        a = f[6][:, 0:n]
        kk = f[7][:, 0:n]
        km = f[9][:, 0:n]
        pa = f[10][:, 0:n]
        Tc = f[12][:, 0:n]
        rkb = b[3][:, 0:n]
        x1 = f[1][:, 0:n]
        eX = f[15][:, 0:n]
        eX3 = f[0][:, 0:n]
        if part == "E":
            self.shift_evac(self.proj(W, 0, 128, lc, n), tbi, pb, rS, mu)
            yield
            self.shift_evac(self.proj(W, 1, 128, lc, n), tbi, 4 + pb, kS, mu)
            yield
            self.shift_evac(self.proj(W, 2, 128, lc, n), tbi, 8 + pb, vS, mu)
            yield
            ps = self.pj()
            self.mm(ps[:, 0:n], self.lora2[0:64, pb * 128:(pb + 1) * 128], self.lora[0:64, lc:lc + n])
            self.act(tneg, ps[:, 0:n], AF.Exp, scale=-1.0, bias=negw0[:, pcol])
            self.act(tneg, tneg, AF.Ln, bias=1.0)
            self.act(tneg, tneg, AF.Exp, scale=-1.0, bias=-0.5)
            yield
            ps = self.pj()
            self.mm(ps[:, 0:n], self.lora2[64:128, pb * 128:(pb + 1) * 128], self.lora[64:128, lc:lc + n])
            self.act(a, ps[:, 0:n], AF.Exp, scale=-1.0, bias=a0[:, pcol])
            self.act(a, a, AF.Ln, bias=1.0)
            self.act(a, a, AF.Exp, scale=-1.0)
            self.ts("dve", kk, kS, kkw[:, pcol], ALU.mult)
            sqb = b[3][:, 0:n]
            self.tt("pool", sqb, kk, kk, ALU.mult)
            yield
            ps = self.pj()
            self.mm(ps[:, 0:n], self.blk64, sqb)
            nrm = f[0][:, 0:n]
            self.act(nrm, ps[:, 0:n], AF.Ln, bias=1e-24)
            self.act(nrm, nrm, AF.Exp, scale=-0.5)
            self.tt("dve", kk, kk, nrm, ALU.mult)
            yield
            self.ts("dve", km, a, ka[:, pcol], ALU.mult, omka[:, pcol], ALU.add)
            self.tt("dve", km, km, kS, ALU.mult)
            self.tt("pool", pa, kk, a, ALU.mult)
            self.stt("dve", rkb, rS, rk[:, pcol], km, ALU.mult, ALU.mult)
            self.scan(Tc, (self.rsts if sample else self.rstp)[:, 0:n], tneg)
            yield
            self.tt("dve", x1, Tc, tneg, ALU.subtract)
            self.act(eX, x1, AF.Exp, scale=-1.0)
            Tc3 = Tc.re("p (s l) -> p s l", l=wsl)
            self.tt("dve", x1.re("p (s l) -> p s l", l=wsl), Tc3[:, :, wsl - 1:wsl].bc([128, nws, wsl]), Tc3,
                    ALU.subtract)
            self.act(eX3, x1, AF.Exp, scale=-1.0)
            self.act(x1, Tc, AF.Exp)
            yield
            return
        eX2 = x1
        gz = ps_["gz"][:, 0:n]
        self.act(gz, self.proj(W, 3, 128, lc, n)[:, 0:n], AF.Silu)
        ps = self.pj()
        self.mm(ps[:, 0:n], self.blk64, rkb)
        bonus = ps_["bonus"][:, 0:n]
        self.tt("dve", bonus, ps[:, 0:n], vS, ALU.mult)
        eL = ps_["eL"][:, 0:n]
        self.act(eL, Tc, AF.Exp, scale=-1.0)
        yield
        QRF = ps_["QRF"]
        PF, KF, PhF, KhF, VF = (ps_[k][:, 0:n] for k in ("PF", "KF", "PhF", "KhF", "VF"))
        self.stt("dve", QRF[:, 0, 0:n], kk, -1.0, eX, ALU.mult, ALU.mult)
        self.tt("dve", QRF[:, 1, 0:n], rS, eL, ALU.mult)
        self.tt("pool", PF, pa, eX2, ALU.mult)
        self.tt("dve", KF, km, eX2, ALU.mult)
        self.tt("dve", PhF, pa, eX3, ALU.mult)
        self.tt("pool", KhF, km, eX3, ALU.mult)
        self.copy("act", VF, vS)
        yield

    def chain(self, *gens):
        for g in gens:
            if g is not None:
                for _ in g:
                    yield

    def merge(self, *gens):
        gens = [g for g in gens if g is not None]
        while gens:
            for g in list(gens):
                try:
                    next(g)
                except StopIteration:
                    gens.remove(g)
            yield

    def flagged(self, gen, flag):
        for _ in gen:
            yield
        flag[0] = True

    def mixB_state(self, it, ps_):
        pb, tbi, W, lc = it
        g0, n, nseg, sl = TBS[tbi]
        sample = nseg > 1
        var = 1 if sample else 0
        nch = n // 128
        sets = []
        for c in range(nch):
            self.chunk_ctr += 1
            sets.append(self.chunk_ctr % 4)
        PF, KF, PhF, KhF, VF = (ps_[k][:, 0:n] for k in ("PF", "KF", "PhF", "KhF", "VF"))
        eL = ps_["eL"][:, 0:n]
        p1 = [self.wkv_p1(sets[c], c, var, PF, KF, PhF, KhF, VF, ps_["QRF"]) for c in range(nch)]
        p2 = [self.wkv_p2(sets[c], pb, c, sample, ps_["QRF"], eL) for c in range(nch)]
        return dict(nch=nch, sample=sample, p1=p1, p2=p2, done1=set())

    def mixB_chunks(self, i, it, st, nst, nflag, last_of_pair):
        pb, tbi, W, lc = it
        sample = st["sample"]
        nch = st["nch"]
        d, o = self.d, self.o
        if sample:
            self.dma("sp", self.Hs, d["swkv"][i, pb].rearrange("p (b e) -> p b e", b=NS))
            self.copy("act", self.Hb, self.Hs)
        else:
            for hh in range(2):
                hp = slice(64 * hh, 64 * hh + 64)
                self.copy("act", self.Hbd[hp, hh * 64:(hh + 1) * 64], self.Hp[hp, pb, :])
        p1, p2, done1 = st["p1"], st["p2"], st["done1"]
        cur = 0
        look = 3 if sample else 4
        while cur < nch:
            if cur in done1:
                try:
                    next(p2[cur])
                except StopIteration:
                    cur += 1
                    continue
            nact = 0
            for idx in range(cur, cur + look):
                if nact >= 3:
                    break
                if idx < nch:
                    if idx not in done1:
                        nact += 1
                        try:
                            next(p1[idx])
                        except StopIteration:
                            done1.add(idx)
                elif nst is not None and nflag[0]:
                    j = idx - nch
                    if j < nst["nch"] and j not in nst["done1"]:
                        nact += 1
                        try:
                            next(nst["p1"][j])
                        except StopIteration:
                            nst["done1"].add(j)
            yield
        if last_of_pair:
            ov = o["owkv"][i, pb].rearrange("p (s e) -> p s e", s=17)
            self.dma("sp", ov[:, 0, :], self.Hp[:, pb, :])
            self.dma("sp", ov[:, 1:17, :], self.Hs)

    def mixB_epi(self, it, ps_, prm_):
        pb, tbi, W, lc = it
        mu, negw0, a0, kkw, ka, omka, rk, gng, gnb = prm_
        g0, n, nseg, sl = TBS[tbi]
        f, b = self.f, self.b
        nch = n // 128
        pcol = slice(pb, pb + 1)
        bonus = ps_["bonus"][:, 0:n]
        gz = ps_["gz"][:, 0:n]
        G = nch * 2
        Ysb = ps_["eL"][:, 0:n]
        Y3 = Ysb.re("p (g e) -> p g e", e=64)
        self.copy("act", Ysb, self.PY[:, 0:n])
        s1 = self.small[:, 0:G]
        self.rsum(s1, Y3)
        self.ts("dve", s1, s1, 1.0 / 64, ALU.mult)
        self.tt("dve", Y3, Y3, s1.un(2).bc([128, G, 64]), ALU.subtract)
        sq = ps_["sqf"][:, 0:n]
        self.tt("dve", sq, Ysb, Ysb, ALU.mult)
        s2 = self.small[:, 8:8 + G]
        self.rsum(s2, sq.re("p (g e) -> p g e", e=64))
        yield
        self.act(s2, s2, AF.Ln, scale=1.0 / 64, bias=64e-5)
        self.act(s2, s2, AF.Exp, scale=-0.5)
        Yn = b[0][:, 0:n]
        self.tt("dve", Yn.re("p (g e) -> p g e", e=64), Y3, s2.un(2).bc([128, G, 64]), ALU.mult)
        yield
        psT = self.pw().bitcast(BF16)
        for c in range(nch):
            self.tr(psT[:, c * 128:(c + 1) * 128], Yn[:, c * 128:(c + 1) * 128], self.identb)
        yb = Ysb
        self.ts("dve", yb, psT[:, 0:n], gng[:, pcol], ALU.mult, gnb[:, pcol], ALU.add)
        self.tt("dve", yb, yb, bonus, ALU.add)
        self.tt("dve", self.yTb[:, pb, lc:lc + n], yb, gz, ALU.mult)

    def run_streams(self, *gens):
        gens = [g for g in gens if g is not None]
        while gens:
            for g in list(gens):
                try:
                    next(g)
                except StopIteration:
                    gens.remove(g)

    def wkv_p1(self, st, c, var, PF, KF, PhF, KhF, VF, QRF):
        cs = slice(c * 128, (c + 1) * 128)
        tok3, AM = self.tok3s[st], self.AMs[st]
        psT = self.pw().bitcast(BF16)
        self.tr(psT[:, 0:128], VF[:, cs], self.identb)
        self.tr(psT[:, 128:256], PhF[:, cs], self.identb)
        self.tr(psT[:, 256:384], KhF[:, cs], self.identb)
        self.copy("act", tok3.re("p a t -> p (a t)"), psT[:, 0:384])
        for hh in range(2):
            hp = slice(64 * hh, 64 * hh + 64)
            bank = self.pw()
            self.mm(bank[:, 0:256].re("p (a t) -> p a t", a=2), PF[hp, cs], QRF[hp, :, cs])
            self.mm(bank[:, 256:512].re("p (a t) -> p a t", a=2), KF[hp, cs], QRF[hp, :, cs])
            self.tt("dve", AM[:, hh, :, :].re("p a t -> p (a t)"), bank, self.maskA[var], ALU.mult)
        yield
        psTn = self.pw().bitcast(BF16)
        for hh in range(2):
            self.tr(psTn[:, hh * 128:(hh + 1) * 128], AM[:, hh, 0, :], self.identb)
        Nping = self.Npings[st]
        Ncur = Nping[0]
        self.copy("act", Ncur.re("p h t -> p (h t)"), psTn[:, 0:256])
        BIG = self.BIGs[st]
        NT = BIG[:, 0, :, :]
        S = BIG[:, 1, :, :]
        self.tt("pool", S, AM[:, :, 0, :], self.identb.un(1).bc([128, 2, 128]), ALU.add)
        yield
        R = 3 if var else 7
        for r in range(1, R + 1):
            NTp = AM[:, :, 0, :] if r == 1 else NT
            Np = Nping[(r - 1) % 2]
            psa = self.pw()
            if r <= R - 2:
                for hh in range(2):
                    self.mm(psa[:, hh * 128:(hh + 1) * 128], Np[:, hh, :], NTp[:, hh, :])
            if r >= 2:
                for hh in range(2):
                    sx = psa[:, 256 + hh * 128:256 + (hh + 1) * 128]
                    self.mm(sx, self.identb, S[:, hh, :], start=True, stop=False)
                    self.mm(sx, Np[:, hh, :], S[:, hh, :], start=False, stop=True)
            if r <= R - 1:
                psb = self.pw()
                for hh in range(2):
                    self.mm(psb[:, hh * 128:(hh + 1) * 128], NTp[:, hh, :], Np[:, hh, :])
            if r == 1:
                self.copy("act", NT.re("p h t -> p (h t)"), psa[:, 0:256])
            elif r <= R - 2:
                self.copy("act", BIG.re("p a h t -> p (a h t)"), psa)
            else:
                self.copy("act", S.re("p h t -> p (h t)"), psa[:, 256:512])
            if r <= R - 1:
                self.copy("act" if r % 2 else "dve", Nping[r % 2].re("p h t -> p (h t)"), psb[:, 0:256])
            yield

    def wkv_p2(self, st, pb, c, sample, QRF, eL):
        nsg = NS if sample else 1
        wsl = 128 // nsg
        tok3, AM = self.tok3s[st], self.AMs[st]
        S7 = self.BIGs[st][:, 1, :, :]
        i64 = self.identb[0:64, 0:64]
        if not sample:
            cols = slice(c * 128, (c + 1) * 128)
            Hbd = self.Hbd
            psX = self.px()
            self.mm(psX[:, 0:128], QRF[:, 0, cols], Hbd, start=True, stop=False)
            for hh in range(2):
                self.mm(psX[:, hh * 64:(hh + 1) * 64], AM[:, hh, 2, :], tok3[:, 0, hh * 64:(hh + 1) * 64],
                        start=False, stop=(hh == 1))
            Xc = self.X[0]
            self.copy("act", Xc, psX[:, 0:128])
            yield
            ps = self.px()
            for hh in range(2):
                self.mm(ps[:, hh * 64:(hh + 1) * 64], S7[:, hh, :], Xc[:, hh * 64:(hh + 1) * 64])
            U = self.X[1]
            self.copy("act", U, ps[:, 0:128])
            yield
            self.mm(self.PY[:, c * 128:(c + 1) * 128], QRF[:, 1, cols], Hbd, start=True, stop=False)
            for hh in range(2):
                ys = self.PY[:, c * 128 + hh * 64:c * 128 + (hh + 1) * 64]
                self.mm(ys, AM[:, hh, 1, :], U[:, hh * 64:(hh + 1) * 64], start=False, stop=False)
                self.mm(ys, AM[:, hh, 3, :], tok3[:, 0, hh * 64:(hh + 1) * 64], start=False, stop=(hh == 1))
            psH = self.px()
            self.mm(psH[:, 0:128], tok3[:, 1, :], U, start=True, stop=False)
            self.mm(psH[:, 0:128], tok3[:, 2, :], tok3[:, 0, :], start=False, stop=True)
            for hh in range(2):
                hp = slice(64 * hh, 64 * hh + 64)
                self.stt("dve", Hbd[hp, hh * 64:(hh + 1) * 64], self.Hp[hp, pb, :],
                         eL[hp, c * 128 + 127:c * 128 + 128], psH[hp, hh * 64:(hh + 1) * 64], ALU.mult, ALU.add)
            for hh in range(2):
                hp = slice(64 * hh, 64 * hh + 64)
                hpb = self.Hp[hp, pb, :]
                self.stt("dve", hpb, hpb, eL[hp, c * 128 + 127:c * 128 + 128], psH[hp, hh * 64:(hh + 1) * 64],
                         ALU.mult, ALU.add)
            yield
            return
        for hh in range(2):
            hp = slice(64 * hh, 64 * hh + 64)
            psZ = self.px()
            pz = psZ[0:64, 0:256].re("p (a t) -> p a t", a=2)
            for sgi in range(nsg):
                cols = slice(c * 128 + sgi * wsl, c * 128 + (sgi + 1) * wsl)
                if nsg == 1:
                    self.mm(pz[:, :, sgi * wsl:(sgi + 1) * wsl], self.Hb[hp, sgi, :], QRF[hp, :, cols])
                else:
                    for a in range(2):
                        self.mm(pz[:, a, sgi * wsl:(sgi + 1) * wsl], self.Hb[hp, sgi, :], QRF[hp, a, cols])
            self.copy("act", self.ZT[:, hh, :, :].re("p a t -> p (a t)"), psZ[0:64, 0:256])
        yield
        psX = self.px()
        for hh in range(2):
            xs = psX[:, hh * 64:(hh + 1) * 64]
            self.mm(xs, self.ZT[:, hh, 0, :], i64, start=True, stop=False)
            self.mm(xs, AM[:, hh, 2, :], tok3[:, 0, hh * 64:(hh + 1) * 64], start=False, stop=True)
        Xc = self.X[0]
        self.copy("act", Xc, psX[:, 0:128])
        yield
        ps = self.px()
        for hh in range(2):
            self.mm(ps[:, hh * 64:(hh + 1) * 64], S7[:, hh, :], Xc[:, hh * 64:(hh + 1) * 64])
        Xn = self.X[1]
        self.copy("act", Xn, ps[:, 0:128])
        Xc = Xn
        yield
        U = Xc
        for hh in range(2):
            ys = self.PY[:, c * 128 + hh * 64:c * 128 + (hh + 1) * 64]
            self.mm(ys, self.ZT[:, hh, 1, :], i64, start=True, stop=False)
            self.mm(ys, AM[:, hh, 1, :], U[:, hh * 64:(hh + 1) * 64], start=False, stop=False)
            self.mm(ys, AM[:, hh, 3, :], tok3[:, 0, hh * 64:(hh + 1) * 64], start=False, stop=True)
        if not sample:
            psH = self.px()
            self.mm(psH[:, 0:128], tok3[:, 1, :], U, start=True, stop=False)
            self.mm(psH[:, 0:128], tok3[:, 2, :], tok3[:, 0, :], start=False, stop=True)
            for hh in range(2):
                hp = slice(64 * hh, 64 * hh + 64)
                hpb = self.Hp[hp, pb, :]
                self.stt("dve", hpb, hpb, eL[hp, c * 128 + 127:c * 128 + 128], psH[hp, hh * 64:(hh + 1) * 64],
                         ALU.mult, ALU.add)
            self.copy("act", self.Hb[:, 0, :], self.Hp[:, pb, :])
            yield
        else:
            eL3 = eL.re("p (s l) -> p s l", l=wsl)
            Um, Vm = self.UmVm[(st + 3) % 4]
            for q4 in range(4):
                sgs = slice(q4 * 4, q4 * 4 + 4)
                self.tt("dve", Um, U.un(1).bc([128, 4, 128]), self.segm[:, sgs].un(2).bc([128, 4, 128]),
                        ALU.mult)
                self.tt("dve", Vm, tok3[:, 0, :].un(1).bc([128, 4, 128]),
                        self.segm[:, sgs].un(2).bc([128, 4, 128]), ALU.mult)
                psH = self.px()
                for s4 in range(4):
                    hs = psH[:, s4 * 128:(s4 + 1) * 128]
                    self.mm(hs, tok3[:, 1, :], Um[:, s4, :], start=True, stop=False)
                    self.mm(hs, tok3[:, 2, :], Vm[:, s4, :], start=False, stop=True)
                p3 = psH.re("p (s e) -> p s e", s=4)
                for hh in range(2):
                    hp = slice(64 * hh, 64 * hh + 64)
                    hsb = self.Hs[hp, sgs, :]
                    self.tt("dve", hsb, hsb, eL3[hp, sgs, wsl - 1:wsl].bc([64, 4, 64]), ALU.mult)
                    self.tt("dve", hsb, hsb, p3[hp, :, hh * 64:(hh + 1) * 64], ALU.add)
                yield

    def odd_layer(self, i):
        d, o = self.d, self.o
        prm = self.prm
        f, b = self.f, self.b
        self.dma("sp", prm[:, 0:NPO], d["po"][i])
        normw = prm[:, PO_NORM:PO_NORM + 8]
        pscale = prm[:, PO_PSC:PO_PSC + 4]
        self.dma("sp", self.lnrow, d["lnrow"][i].partition_broadcast(128))
        lng = self.lnrow[:, 0:512]
        lnb = self.lnrow[:, 512:1024]
        spb = self.lnrow[:, 1024:1536].re("p (g t) -> p g t", g=4)
        self.dma("pool", self.poolw.re("p g d -> p (g d)"), d["poolw"][i])
        spT = f[0][:, 0:512]
        self.dma("sp", spT, d["spT"][i])
        maskI = self.maskA[0][:, 128:256]
        self.tt("dve", self.WsT, spT.re("p (g t) -> p g t", g=4), maskI.un(1).bc([128, 4, 128]), ALU.mult)
        ps = self.pj()
        self.mm(ps[:, 0:32].re("p (g t) -> p g t", g=4), self.rep8, spT[0:8, :].re("p (g t) -> p g t", g=4)[:, :, 0:8])
        rep = f[1][:, 0:32]
        self.copy("act", rep, ps[:, 0:32])
        maskIs = self.maskA[1][:, 128:256]
        for g in range(4):
            self.tt("dve", self.WsTs[:, g, :].re("p (b t) -> p b t", b=NS),
                    rep[:, g * 8:(g + 1) * 8].un(1).bc([128, NS, 8]),
                    maskIs.re("p (b t) -> p b t", b=NS), ALU.mult)
        self.memset("dve", self.pc, 0.0)
        wdr = d["w_in_odd"][i]
        for tg in TGS:
            self.norm(tg, normw)
            self.cp("o_norm")
            Wv = self.wload(wdr, [1536], 512)
            chunks = []
            lc = 0
            for tbi in tg:
                g0, n, nseg, sl = TBS[tbi]
                for c in range(n // 128):
                    chunks.append((lc + c * 128, nseg > 1))
                lc += n
            self.run_window([self.d1_chunk(i, cl, smp, Wv, ci % 5, lng, lnb, spb)
                             for ci, (cl, smp) in enumerate(chunks)], 5)
            self.cp("o_d1")
            for g in range(4):
                W = self.wload(wdr, [1024 + 128 * g, 2048 + 128 * g], 128)
                lc = 0
                for tbi in tg:
                    g0, n, nseg, sl = TBS[tbi]
                    ps_z = self.proj(W, 1, 128, lc, n)
                    sz = f[3][:, 0:n]
                    self.act(sz, ps_z[:, 0:n], AF.Silu)
                    ps_u = self.proj(W, 0, 128, lc, n)
                    t = f[4][:, 0:n]
                    self.tt("dve", t, ps_u[:, 0:n], self.mixT[:, g, lc:lc + n], ALU.mult)
                    self.tt("dve", self.yTb[:, g, lc:lc + n], t, sz, ALU.mult)
                    lc += n
            self.cp("o_d2")
            Wg = {}
            calls = []
            ci = 0
            for g in range(4):
                lc = 0
                for ti, tbi in enumerate(tg):
                    calls.append(self.mixerC_gen(i, g, tbi, lc, pscale, ci % 2, Wg, wdr, ti == 0, 4 in tg))
                    lc += TBS[tbi][1]
                    ci += 1
            self.run_window(calls, 2)
            self.cp("o_c")
            self.outproj(tg, d["w_out_odd"][i])
            self.cp("o_out")

    def run_window(self, gens, width):
        pending = list(gens)
        active = []
        while pending or active:
            while pending and len(active) < width:
                active.append(pending.pop(0))
            for g in list(active):
                try:
                    next(g)
                except StopIteration:
                    active.remove(g)

    def d1_sets(self):
        if not hasattr(self, "_d1sets"):
            f, b = self.f, self.b
            amf = [B(self.AMs[k].ap.rearrange("p h a t -> p (h a t)").bitcast(F32), "AM_%d" % k) for k in range(4)]
            qrff = B(self.QRF.ap.rearrange("p a t -> p (a t)").bitcast(F32), "QRF")
            hs = self.Hs.re("p b e -> p (b e)")
            self._d1sets = [
                (f[3][:, 0:512], f[4][:, 0:512], f[5][:, 0:512], b[0][:, 0:512]),
                (f[7][:, 0:512], amf[0], amf[1], b[2][:, 0:512]),
                (amf[2], hs[:, 0:512], hs[:, 512:1024], b[4][:, 0:512]),
                (f[6][:, 0:512], amf[3], qrff, b[5][:, 0:512]),
                (f[0][:, 0:512], f[1][:, 0:512], f[2][:, 0:512], b[6][:, 0:512]),
            ]
        return self._d1sets

    def d1_chunk(self, i, cl, sample, Wv, sset, lng, lnb, spb):
        o = self.o
        v, vc, vn, vnb = self.d1_sets()[sset]
        ss = self.small[:, 16 + 2 * sset:18 + 2 * sset]
        s1 = self.small[:, 16 + 2 * sset:17 + 2 * sset]
        s2 = self.small[:, 17 + 2 * sset:18 + 2 * sset]
        ps = self.pj()
        for k in range(8):
            self.mm(ps, self.hT[:, k, cl:cl + 128], Wv[:, k, :], start=(k == 0), stop=(k == 7))
        self.memset("dve", ss, 0.0)
        self.act(v, ps, AF.Identity, accum=s1)
        yield
        self.ts("dve", s1, s1, 1.0 / 512, ALU.mult)
        self.ts("dve", vc, v, s1, ALU.subtract)
        yield
        self.act(v, vc, AF.Square, accum=s2)
        self.act(s2, s2, AF.Ln, scale=1.0 / 512, bias=1e-5)
        self.act(s2, s2, AF.Exp, scale=-0.5)
        yield
        self.stt("dve", vn, vc, s2, lng, ALU.mult, ALU.mult)
        self.tt("dve", vn, vn, lnb, ALU.add)
        yield
        self.copy("act", vnb, vn)
        if sample:
            self.dma("sp", o["ochunkv"][i], vn)
        yield
        psM = self.pw()
        Ws = self.WsTs if sample else self.WsT
        for g in range(4):
            self.mm(psM[:, g * 128:(g + 1) * 128], vnb[:, g * 128:(g + 1) * 128], Ws[:, g, :])
        if sample:
            self.tt("dve", self.mixT[:, :, cl:cl + 128].re("p g (b t) -> p g b t", b=NS),
                    psM.re("p (g b t) -> p g b t", g=4, b=NS),
                    spb[:, :, 0:8].un(2).bc([128, 4, NS, 8]), ALU.add)
        else:
            self.tt("dve", self.mixT[:, :, cl:cl + 128], psM.re("p (g t) -> p g t", g=4), spb, ALU.add)

    def mixerC_gen(self, i, g, tbi, lc, pscale, sset, Wg, wdr, first, has_sample):
        g0, n, nseg, sl = TBS[tbi]
        f, b = self.f, self.b
        o, d = self.o, self.d
        if sset == 0:
            pextb, bufs, szb, plbb, tmpo = f[0], [f[1], f[2]], f[3], b[0], 32
        else:
            pextb, bufs, szb, plbb, tmpo = f[4], [f[5], f[6]], f[7], b[2], 48
        if first:
            Wg[g] = self.wload(wdr, [128 * g, 512 + 128 * g], 128)
            if has_sample:
                self.dma("sp", self.spool, d["spool"][i][:, g, :].rearrange("p (b r) -> p b r", b=NS))
        W = Wg[g]
        w = 2 << g
        L = sl + 15
        pext = pextb[:, 0:nseg * L].re("p (s l) -> p s l", l=L)
        ov = o["opool"][i, g].rearrange("p (s r) -> p s r", s=17)
        ps_p = self.proj(W, 0, 128, lc, n)
        self.copy("act", pext[:, :, 15:L], ps_p[:, 0:n].re("p (s l) -> p s l", l=sl))
        if nseg == 1:
            self.copy("dve", pext[:, 0, 0:15], self.pc[:, g, :])
            self.copy("act", self.pc[:, g, :], pext[:, 0, sl:sl + 15])
            if g0 + n == SEQ:
                self.dma("sp", ov[:, 0, :], pext[:, 0, sl:sl + 15])
        else:
            self.copy("dve", pext[:, :, 0:15], self.spool)
            self.dma("sp", ov[:, 1:17, :], pext[:, :, sl:sl + 15])
        ps_cz = self.proj(W, 1, 128, lc, n)
        sz = szb[:, 0:n]
        self.act(sz, ps_cz[:, 0:n], AF.Silu)
        yield
        cur = pext
        step = 1
        Lc = L
        for s_ in range(g + 1):
            nxt = bufs[s_ % 2][:, 0:nseg * L].re("p (s l) -> p s l", l=L)
            Ln = Lc - step
            self.tt("dve", nxt[:, :, 0:Ln], cur[:, :, step:Lc], cur[:, :, 0:Ln], ALU.add)
            cur, Lc, step = nxt, Ln, step * 2
        off = 16 - w
        win = cur[:, :, off:off + sl]
        plb = plbb[:, 0:n]
        self.stt("dve", plb.re("p (s l) -> p s l", l=sl), win, 1.0 / w, pext[:, :, 15:L], ALU.mult, ALU.subtract)
        if g0 == 0:
            tmp = self.small[:, tmpo:tmpo + w - 1]
            self.tt("dve", tmp, win[:, 0, 0:w - 1], self.invcnt[:, 0:w - 1], ALU.mult)
            self.tt("dve", plb[:, 0:w - 1], tmp, pext[:, 0, 15:15 + w - 1], ALU.subtract)
        yield
        ps_y = self.pj()
        self.mm(ps_y[:, 0:n], self.poolw[:, g, :], plb)
        self.stt("dve", self.yTa[:, g, lc:lc + n], ps_y[:, 0:n], pscale[:, g:g + 1], sz, ALU.mult, ALU.mult)


C_IDENT = 0
C_ONES = 128
C_BLK64 = 256
C_MASKA_P = 384
C_MASKA_S = 896
C_MASKN_P = 1408
C_MASKN_S = 1664
C_SEGM = 1920
CB_COLS = 1936
C_RSTP = 1936
C_RSTS = 2448
C_INVC = 2576
C_REP8 = 2592
CONST_COLS = 2720

PE_NORM, PE_CW, PE_MU, PE_W0, PE_A0, PE_KK, PE_KA, PE_RK, PE_GNG, PE_GNB = 0, 8, 20, 33, 37, 41, 45, 49, 53, 57
NPE = 61
PO_NORM, PO_PSC = 0, 8
NPO = 12


def make_consts():
    c = np.zeros((128, CONST_COLS), np.float32)
    idx = np.arange(128)
    c[:, C_IDENT:C_IDENT + 128] = np.eye(128)
    c[:, C_ONES:C_ONES + 128] = 1.0
    c[:, C_BLK64:C_BLK64 + 128] = (idx[:, None] // 64 == idx[None, :] // 64)
    s, t = idx[:, None], idx[None, :]
    for var, base_a, base_n in ((0, C_MASKA_P, C_MASKN_P), (1, C_MASKA_S, C_MASKN_S)):
        same = np.ones((128, 128), bool) if var == 0 else (s // 8 == t // 8)
        mts = (same & (s < t)).astype(np.float32)
        mti = (same & (s <= t)).astype(np.float32)
        mls = (same & (t < s)).astype(np.float32)
        for hh in range(2):
            c[:, base_a + hh * 256:base_a + hh * 256 + 128] = mts
            c[:, base_a + hh * 256 + 128:base_a + hh * 256 + 256] = mti
            c[:, base_n + hh * 128:base_n + (hh + 1) * 128] = mls
    c[:, C_SEGM:C_SEGM + NS] = (idx[:, None] // 8 == np.arange(NS)[None, :])
    rp = np.ones(512, np.float32)
    rp[::128] = 0.0
    c[:, C_RSTP:C_RSTP + 512] = rp[None, :]
    rs = np.ones(128, np.float32)
    rs[::8] = 0.0
    c[:, C_RSTS:C_RSTS + 128] = rs[None, :]
    c[:, C_INVC:C_INVC + 15] = (1.0 / np.arange(1, 16, dtype=np.float64)).astype(np.float32)[None, :]
    c[0:8, C_REP8:C_REP8 + 128] = (np.arange(8)[:, None] == (idx[None, :] % 8))
    return c


_CACHE = {}


def _fm(v, nchunk):
    return np.ascontiguousarray(np.asarray(v, np.float32).reshape(nchunk, 128).T)


def kernel(x_prompt, x_sample, state_conv, state_shift, state_wkv, state_pool,
           norm_w, final_norm_w, w_in_even, conv_w, shift_mu, w0, w2, a0, a2, k_k, k_a, r_k,
           gn_g, gn_b, w_out_even, w_in_odd, pool_w, pool_scale, v_ln_g, v_ln_b,
           spatial_w, spatial_b, w_out_odd):
    f32 = lambda a: np.ascontiguousarray(np.asarray(a, np.float32))
    x_prompt, x_sample = f32(x_prompt), f32(x_sample)
    state_conv, state_shift, state_wkv, state_pool = map(f32, (state_conv, state_shift, state_wkv, state_pool))
    if "nc" not in _CACHE:
        kb = K()
        stats = kb.build()
        _CACHE["nc"] = kb.nc
        _CACHE["stats"] = stats
    nc = _CACHE["nc"]
    pe = np.zeros((2, 128, NPE), np.float32)
    po = np.zeros((2, 128, NPO), np.float32)
    lora2 = np.zeros((2, 128, 512), np.float32)
    lnrow = np.zeros((2, 1536), np.float32)
    for i in range(2):
        pe[i, :, PE_NORM:PE_NORM + 8] = _fm(norm_w[2 * i], 8)
        cwm = np.asarray(conv_w[i], np.float32)
        pe[i, :, PE_CW:PE_CW + 12] = cwm.reshape(3, 4, 128).transpose(2, 1, 0).reshape(128, 12)
        pe[i, :, PE_MU:PE_MU + 13] = _fm(shift_mu[i], 13)
        for nm, arr in ((PE_W0, w0), (PE_A0, a0), (PE_KK, k_k), (PE_KA, k_a), (PE_GNG, gn_g), (PE_GNB, gn_b)):
            pe[i, :, nm:nm + 4] = _fm(arr[i], 4)
        pe[i, :, PE_RK:PE_RK + 4] = _fm(np.asarray(r_k[i]).reshape(512), 4)
        lora2[i, 0:64] = np.asarray(w2[i], np.float32)
        lora2[i, 64:128] = np.asarray(a2[i], np.float32)
        po[i, :, PO_NORM:PO_NORM + 8] = _fm(norm_w[2 * i + 1], 8)
        po[i, :, PO_PSC:PO_PSC + 4] = _fm(pool_scale[i], 4)
        lnrow[i, 0:512] = np.asarray(v_ln_g[i], np.float32)
        lnrow[i, 512:1024] = np.asarray(v_ln_b[i], np.float32)
        lnrow[i, 1024:1536] = np.asarray(spatial_b[i], np.float32).reshape(512)
    poolw = f32(np.asarray(pool_w, np.float32).transpose(0, 2, 1, 3).reshape(2, 128, 512))
    spT = f32(np.asarray(spatial_w, np.float32).transpose(0, 3, 1, 2).reshape(2, 128, 512))
    shared = {
        "w_in_even": f32(w_in_even), "w_out_even": f32(w_out_even),
        "w_in_odd": f32(w_in_odd), "w_out_odd": f32(w_out_odd),
        "consts": make_consts(), "pe": pe, "po": po, "lora2": lora2, "fnw": _fm(final_norm_w, 8),
        "lnrow": lnrow, "poolw": poolw, "spT": spT,
    }
    in_maps = []
    for c in range(NCORE):
        sl = slice(NS * c, NS * (c + 1))
        xt = np.concatenate([x_prompt[c].reshape(16, 128, D), x_sample[sl].reshape(1, 128, D)], axis=0)
        m = dict(shared)
        m["xtok"] = f32(xt)
        m["sconv"] = f32(state_conv[:, sl].reshape(2, NS, 2, 4, 128).transpose(0, 4, 3, 1, 2).reshape(2, 128, -1))
        m["sshift"] = f32(state_shift[:, sl].reshape(2, NS, 13, 128).transpose(0, 3, 2, 1).reshape(2, 128, -1))
        m["spool"] = f32(state_pool[:, sl].reshape(2, NS, 15, 4, 128).transpose(0, 4, 3, 1, 2).reshape(2, 128, 4, -1))
        m["swkv"] = f32(state_wkv[:, sl].reshape(2, NS, 4, 2, 64, 64).transpose(0, 2, 3, 5, 1, 4).reshape(2, 4, 128, -1))
        in_maps.append(m)
    res = run_bass_kernel_spmd(nc, in_maps, core_ids=list(range(NCORE)))
    R = res.results
    y_prompt = np.zeros((8, SEQ, D), np.float32)
    y_sample = np.zeros((128, ST, D), np.float32)
    conv_p = np.zeros((2, 8, 2, 512), np.float32)
    conv_s = np.zeros((2, 128, 2, 512), np.float32)
    shift_p = np.zeros((2, 8, 1664), np.float32)
    shift_s = np.zeros((2, 128, 1664), np.float32)
    wkv_p = np.zeros((2, 8, 8, 64, 64), np.float32)
    wkv_s = np.zeros((2, 128, 8, 64, 64), np.float32)
    pool_p = np.zeros((2, 8, 15, 512), np.float32)
    pool_s = np.zeros((2, 128, 15, 512), np.float32)
    chunkv = np.zeros((2, 128, ST, 512), np.float32)
    for c in range(NCORE):
        r = R[c]
        sl = slice(NS * c, NS * (c + 1))
        yt = np.asarray(r["ytok"])
        y_prompt[c] = yt[0:16].reshape(SEQ, D)
        y_sample[sl] = yt[16].reshape(NS, ST, D)
        oc = np.asarray(r["oconv"]).reshape(2, 128, 4, 17, 2)
        oc = oc.transpose(0, 3, 4, 2, 1).reshape(2, 17, 2, 512)
        conv_p[:, c] = oc[:, 0]
        conv_s[:, sl] = oc[:, 1:]
        osf = np.asarray(r["oshift"]).reshape(2, 128, 13, 17).transpose(0, 3, 2, 1).reshape(2, 17, 1664)
        shift_p[:, c] = osf[:, 0]
        shift_s[:, sl] = osf[:, 1:]
        ow = np.asarray(r["owkv"]).reshape(2, 4, 2, 64, 17, 64)
        ow = ow.transpose(0, 4, 1, 2, 5, 3).reshape(2, 17, 8, 64, 64)
        wkv_p[:, c] = ow[:, 0]
        wkv_s[:, sl] = ow[:, 1:]
        op = np.asarray(r["opool"]).reshape(2, 4, 128, 17, 15)
        op = op.transpose(0, 3, 4, 1, 2).reshape(2, 17, 15, 512)
        pool_p[:, c] = op[:, 0]
        pool_s[:, sl] = op[:, 1:]
        chunkv[:, sl] = np.asarray(r["ochunkv"]).reshape(2, NS, ST, 512)
    return (y_prompt, y_sample, conv_p, conv_s, shift_p, shift_s, wkv_p, wkv_s, pool_p, pool_s, chunkv)
```
